# Optimizing a Trainium2 kernel written in Bass

```python
import jax, jax.numpy as jnp
from jax import lax
import numpy as np

D_MODEL = 1024
BATCH = 4
SEQ = 8192
DEPTH = 2

N_META = 16
HGRN_HEAD_V = 128
HGRN_WIDTH = D_MODEL // 2
HGRN_HEADS = HGRN_WIDTH // HGRN_HEAD_V
HGRN_EXPAND = 128
HGRN_KDIM = HGRN_HEADS * HGRN_EXPAND
CHUNK = 64
POOL_WINDOWS = (2, 4, 8, 16)
POOL_GROUPS = len(POOL_WINDOWS)
POOL_WIDTH = D_MODEL // 2
POOL_GROUP_DIM = POOL_WIDTH // POOL_GROUPS
D_FF = 2816
CONV_WIDTH = 3
EPS = 1e-6
LOG_FLOOR = 1e-30
SPLITS = (HGRN_KDIM, HGRN_KDIM, HGRN_WIDTH, HGRN_WIDTH, POOL_WIDTH, D_MODEL, D_MODEL)
IN_COLS = sum(SPLITS)

kernel_name = "hybrid_hgrn2_pool_convffn_trunk"


def rmsnorm(x, gain):
    xf = x.astype(jnp.float32)
    y = xf * lax.rsqrt(jnp.mean(xf * xf, axis=-1, keepdims=True) + EPS)
    return (y * gain.astype(jnp.float32)).astype(x.dtype)


def _hgrn2_chunk_step(state, xs):
    q, k, v, lf = xs
    c = q.shape[2]
    b = jnp.cumsum(lf, axis=2)
    o_inter = jnp.einsum('bhtk,bhkv->bhtv', q * jnp.exp(b), state)
    diff = b[:, :, :, None, :] - b[:, :, None, :, :]
    causal = (jnp.arange(c)[:, None] >= jnp.arange(c)[None, :])[None, None, :, :, None]
    decay = jnp.where(causal, jnp.exp(jnp.where(causal, diff, 0.0)), 0.0)
    scores = jnp.einsum('bhtk,bhtsk,bhsk->bhts', q, decay, k)
    o_intra = jnp.einsum('bhts,bhsv->bhtv', scores, v)
    b_last = b[:, :, -1, :]
    k_dec = k * jnp.exp(b_last[:, :, None, :] - b)
    new_state = jnp.exp(b_last)[..., None] * state + jnp.einsum('bhsk,bhsv->bhkv', k_dec, v)
    return new_state, o_inter + o_intra


def hgrn2(q, f_logit, i, g, lb, out_gain):
    bsz, L, _ = q.shape
    lb = lb.astype(jnp.float32)
    fl = f_logit.astype(jnp.float32)
    f = lb + (1.0 - lb) * jax.nn.sigmoid(fl)
    log_f = jnp.log(jnp.maximum(f, LOG_FLOOR))
    k = (1.0 - lb) * jax.nn.sigmoid(-fl)
    pad = CHUNK - N_META
    def prep(t, d):
        t = jnp.pad(t.astype(jnp.float32), ((0, 0), (pad, 0), (0, 0)))
        n = t.shape[1] // CHUNK
        return t.reshape(bsz, n, CHUNK, HGRN_HEADS, d).transpose(1, 0, 3, 2, 4)
    xs = (prep(q, HGRN_EXPAND), prep(k, HGRN_EXPAND), prep(i, HGRN_HEAD_V), prep(log_f, HGRN_EXPAND))
    state0 = jnp.zeros((bsz, HGRN_HEADS, HGRN_EXPAND, HGRN_HEAD_V), jnp.float32)
    _, o = lax.scan(_hgrn2_chunk_step, state0, xs)
    o = o.transpose(1, 0, 3, 2, 4).reshape(bsz, -1, HGRN_HEADS, HGRN_HEAD_V)[:, pad:]
    o = o * lax.rsqrt(jnp.mean(o * o, axis=-1, keepdims=True) + EPS)
    o = o.reshape(bsz, L, HGRN_WIDTH) * out_gain.astype(jnp.float32)
    return o * jax.nn.sigmoid(g.astype(jnp.float32))


def multiscale_pool(v, proj, scale):
    bsz, L, _ = v.shape
    vg = v.astype(jnp.float32).reshape(bsz, L, POOL_GROUPS, POOL_GROUP_DIM)
    cs0 = jnp.pad(jnp.cumsum(vg, axis=1), ((0, 0), (1, 0), (0, 0), (0, 0)))
    t1 = jnp.arange(1, L + 1)
    outs = []
    for gi, w in enumerate(POOL_WINDOWS):
        hi = cs0[:, 1:, gi]
        lo = jnp.pad(cs0[:, :L + 1 - w, gi], ((0, 0), (w - 1, 0), (0, 0)))
        cnt = jnp.minimum(t1, w).astype(jnp.float32)[None, :, None]
        outs.append((hi - lo) / cnt - vg[:, :, gi])
    pooled = jnp.stack(outs, axis=2)
    y = jnp.einsum('blgc,gcd->blgd', pooled, proj.astype(jnp.float32))
    return y.reshape(bsz, L, POOL_WIDTH) * scale.astype(jnp.float32)


def causal_dwconv(x, w, b):
    L = x.shape[1]
    xp = jnp.pad(x, ((0, 0), (CONV_WIDTH - 1, 0), (0, 0)))
    y = b
    for j in range(CONV_WIDTH):
        y = y + xp[:, j:j + L] * w[j]
    return y


def setup_inputs(seed: int = 0) -> dict:
    key = jax.random.key(seed)
    ks = jax.random.split(key, 24)
    nrm = lambda k, shape, s: jax.random.normal(k, shape, jnp.float32) * s
    gain = lambda k, shape: 1.0 + nrm(k, shape, 0.05)
    return {
        "x": nrm(ks[0], (BATCH, SEQ, D_MODEL), 1.0),
        "meta_tokens": nrm(ks[1], (N_META, D_MODEL), 1.0),
        "mix_norm_pre": gain(ks[2], (DEPTH, D_MODEL)),
        "mix_norm_post": gain(ks[3], (DEPTH, D_MODEL)),
        "w_in": nrm(ks[4], (DEPTH, D_MODEL, IN_COLS), D_MODEL ** -0.5),
        "hgrn_lower_bounds": nrm(ks[5], (DEPTH, HGRN_KDIM), 0.5),
        "hgrn_out_norm": gain(ks[6], (DEPTH, HGRN_WIDTH)),
        "w_branch_hgrn": nrm(ks[7], (DEPTH, HGRN_WIDTH, D_MODEL), HGRN_WIDTH ** -0.5),
        "pool_proj": nrm(ks[8], (DEPTH, POOL_GROUPS, POOL_GROUP_DIM, POOL_GROUP_DIM), POOL_GROUP_DIM ** -0.5),
        "pool_scale": gain(ks[9], (DEPTH, POOL_WIDTH)),
        "w_branch_pool": nrm(ks[10], (DEPTH, POOL_WIDTH, D_MODEL), POOL_WIDTH ** -0.5),
        "w_out": nrm(ks[11], (DEPTH, D_MODEL, D_MODEL), D_MODEL ** -0.5),
        "ffn_norm_pre": gain(ks[12], (DEPTH, D_MODEL)),
        "ffn_norm_post": gain(ks[13], (DEPTH, D_MODEL)),
        "ffn_w_gate": nrm(ks[14], (DEPTH, D_MODEL, D_FF), D_MODEL ** -0.5),
        "ffn_w_up": nrm(ks[15], (DEPTH, D_MODEL, D_FF), D_MODEL ** -0.5),
        "ffn_conv_w": nrm(ks[16], (DEPTH, CONV_WIDTH, D_FF), CONV_WIDTH ** -0.5),
        "ffn_conv_b": nrm(ks[17], (DEPTH, D_FF), 0.02),
        "ffn_w_down": nrm(ks[18], (DEPTH, D_FF, D_MODEL), D_FF ** -0.5),
    }


def reference(x, meta_tokens, mix_norm_pre, mix_norm_post, w_in, hgrn_lower_bounds, hgrn_out_norm,
              w_branch_hgrn, pool_proj, pool_scale, w_branch_pool, w_out, ffn_norm_pre, ffn_norm_post,
              ffn_w_gate, ffn_w_up, ffn_conv_w, ffn_conv_b, ffn_w_down):
    bsz = x.shape[0]
    meta = jnp.broadcast_to(meta_tokens[None].astype(x.dtype), (bsz, N_META, D_MODEL))
    h = jnp.concatenate([meta, x], axis=1)
    gam = jax.nn.softmax(hgrn_lower_bounds.astype(jnp.float32), axis=0)
    lbs = jnp.clip(jnp.cumsum(gam, axis=0) - gam[0], 0.0, 1.0)
    cuts = list(np.cumsum(SPLITS)[:-1])
    for l in range(DEPTH):
        u = rmsnorm(h, mix_norm_pre[l])
        proj = u @ w_in[l]
        q, f_logit, i_in, g_out, v_pool, gate_a, gate_b = jnp.split(proj, cuts, axis=-1)
        a = hgrn2(q, f_logit, i_in, g_out, lbs[l], hgrn_out_norm[l])
        p = multiscale_pool(v_pool, pool_proj[l], pool_scale[l])
        z = (jax.nn.sigmoid(gate_a.astype(jnp.float32)) * (a @ w_branch_hgrn[l])
             + jax.nn.sigmoid(gate_b.astype(jnp.float32)) * (p @ w_branch_pool[l]))
        h = h + rmsnorm(z @ w_out[l], mix_norm_post[l])
        u = rmsnorm(h, ffn_norm_pre[l])
        gt = causal_dwconv(u @ ffn_w_gate[l], ffn_conv_w[l], ffn_conv_b[l])
        y = (jax.nn.gelu(gt, approximate=True) * (u @ ffn_w_up[l])) @ ffn_w_down[l]
        h = h + rmsnorm(y, ffn_norm_post[l])
    return h[:, N_META:]
```

```python
import contextlib
import numpy as np
import concourse.bass as bass
import concourse.mybir as mybir
from concourse.bass_utils import run_bass_kernel_spmd

F32 = mybir.dt.float32
BF16 = mybir.dt.bfloat16
AF = mybir.ActivationFunctionType
ALU = mybir.AluOpType

D = 1024
NDC = 8
TT = 512
DFF = 2816
NFC = 22
NH = 4
INC = 4608
L = 2
SEQ = 8192
NMETA = 16
T0 = 128
EPS = 1e-6
NSLOT = 6
PIECE = 4096
NPIECE = 32

C_R0, C_R1, C_OG, C_PS, C_MPRE, C_MPOST, C_FPRE, C_FPOST, C_CW, C_CB = 0, 4, 8, 12, 16, 24, 32, 40, 48, 114
C_LB, C_OML, C_NOML = 136, 140, 144
NCST = 148


class Tok:
    __slots__ = ("sem", "val")

    def __init__(self, sem, val):
        self.sem = sem
        self.val = val


class Trk:
    ENG = ("pe", "act", "dve", "pool", "sp")

    def __init__(self, nc, stack):
        self.nc = nc
        self.stack = stack
        self.thunks = {e: [] for e in self.ENG}
        self.cnt = {e: 0 for e in self.ENG}
        self.psem = {e: stack.enter_context(nc.semaphore("prog_" + e)) for e in self.ENG}
        self.waited = {e: {} for e in self.ENG}
        self.lastw = {}
        self.rds = {}
        self.dsem = {}
        self.nwaits = 0

    def _deps(self, reads, writes):
        deps = []
        for r in reads:
            t = self.lastw.get(r)
            if t is not None:
                deps.append(t)
        for w in writes:
            t = self.lastw.get(w)
            if t is not None:
                deps.append(t)
            deps.extend(self.rds.get(w, {}).values())
        return deps

    def _waits(self, eng, deps):
        need = {}
        for t in deps:
            if eng == "pe" and t.sem is self.psem["pe"]:
                continue
            k = id(t.sem)
            if k not in need or need[k][1] < t.val:
                need[k] = (t.sem, t.val)
        out = []
        for k, (sem, val) in need.items():
            if self.waited[eng].get(k, 0) >= val:
                continue
            self.waited[eng][k] = val
            out.append((sem, val))
        self.nwaits += len(out)
        return out

    def _record(self, tok, reads, writes):
        for r in reads:
            d = self.rds.setdefault(r, {})
            k = id(tok.sem)
            if k not in d or d[k].val < tok.val:
                d[k] = tok
        for w in writes:
            self.lastw[w] = tok
            self.rds[w] = {}

    def op(self, eng, fn, reads=(), writes=(), signal=True):
        waits = self._waits(eng, self._deps(reads, writes))
        if signal:
            self.cnt[eng] += 1
            tok = Tok(self.psem[eng], self.cnt[eng])
        else:
            tok = Tok(self.psem[eng], self.cnt[eng] + 1)
        sem = self.psem[eng]

        def thunk(e, fn=fn, waits=waits, signal=signal, sem=sem):
            for (s, v) in waits:
                e.wait_ge(s, v)
            ins = fn(e)
            if signal:
                ins.then_inc(sem, 1)

        self.thunks[eng].append(thunk)
        self._record(tok, reads, writes)
        return tok

    def dma(self, q, out, in_, reads=(), writes=(), key=None, record=True):
        waits = self._waits(q, self._deps(reads, writes))
        if key not in self.dsem:
            self.dsem[key] = [self.stack.enter_context(self.nc.semaphore("dma_%s" % (str(key).replace(" ", "")))), 0]
        s = self.dsem[key]
        s[1] += 16
        tok = Tok(s[0], s[1])
        sem = s[0]

        def thunk(e, out=out, in_=in_, waits=waits, sem=sem):
            for (sm, v) in waits:
                e.wait_ge(sm, v)
            e.dma_start(out=out, in_=in_).then_inc(sem, 16)

        self.thunks[q].append(thunk)
        if record:
            self._record(tok, reads, writes)
        return tok

    def wait_tok(self, eng, tok):
        waits = self._waits(eng, [tok])

        def thunk(e, waits=waits):
            for (s, v) in waits:
                e.wait_ge(s, v)

        self.thunks[eng].append(thunk)


def build_nc(k1x=8, k2=8, nlayers=L, stop=99):
    nc = bass.Bass("TRN2", target_bir_lowering=False)
    dr = {}
    dr["x1"] = nc.dram_tensor("x1", [max(k1x, 1) * TT, D], F32, kind="ExternalInput").ap()
    dr["meta"] = nc.dram_tensor("meta", [T0, D], F32, kind="ExternalInput").ap()
    dr["x2"] = nc.dram_tensor("x2", [k2 * TT, D], F32, kind="ExternalInput").ap()
    dr["w_in"] = nc.dram_tensor("w_in", [L, D, INC], F32, kind="ExternalInput").ap()
    dr["w_bh"] = nc.dram_tensor("w_bh", [L, 512, D], F32, kind="ExternalInput").ap()
    dr["w_bp"] = nc.dram_tensor("w_bp", [L, 512, D], F32, kind="ExternalInput").ap()
    dr["w_out"] = nc.dram_tensor("w_out", [L, D, D], F32, kind="ExternalInput").ap()
    dr["w_gate"] = nc.dram_tensor("w_gate", [L, D, DFF], F32, kind="ExternalInput").ap()
    dr["w_up"] = nc.dram_tensor("w_up", [L, D, DFF], F32, kind="ExternalInput").ap()
    dr["w_down"] = nc.dram_tensor("w_down", [L, DFF, D], F32, kind="ExternalInput").ap()
    dr["pproj"] = nc.dram_tensor("pproj", [L, 4, 128, 128], F32, kind="ExternalInput").ap()
    dr["cst"] = nc.dram_tensor("cst", [L, 128, NCST], F32, kind="ExternalInput").ap()
    dr["ident"] = nc.dram_tensor("ident", [128, 128], F32, kind="ExternalInput").ap()
    dr["trimask"] = nc.dram_tensor("trimask", [128, 128], F32, kind="ExternalInput").ap()
    dr["scanmask"] = nc.dram_tensor("scanmask", [128, TT], F32, kind="ExternalInput").ap()
    dr["invc1"] = nc.dram_tensor("invc1", [128, 4, T0], F32, kind="ExternalInput").ap()
    dr["invc2"] = nc.dram_tensor("invc2", [128, 4, TT], F32, kind="ExternalInput").ap()
    out = nc.dram_tensor("out", [k2 * TT, D], F32, kind="ExternalOutput").ap()
    wscr = nc.dram_tensor("wscr", [L, NPIECE, 128, PIECE], BF16, kind="Internal").ap()

    with contextlib.ExitStack() as st:
        T = Trk(nc, st)

        def sb(name, shape, dt):
            return st.enter_context(nc.sbuf_tensor(name, shape, dt))

        h = sb("h", [128, NDC, TT], F32)
        u = sb("u", [128, NDC, TT], BF16)
        sqz = sb("sqz", [128, NDC, TT], BF16)
        FQ = sb("FQ", [128, 4, TT], F32)
        FS = sb("FS", [128, 4, TT], F32)
        FK = sb("FK", [128, 4, TT], F32)
        FB = sb("FB", [128, 4, TT], F32)
        FE = sb("FE", [128, 4, TT], F32)
        BQ = sb("BQ", [128, 4, TT], BF16)
        BK = sb("BK", [128, 4, TT], BF16)
        BD = sb("BD", [128, 4, TT], BF16)
        BKD = sb("BKD", [128, 4, TT], BF16)
        BV = sb("BV", [128, 4, TT], BF16)
        BG = sb("BG", [128, 4, TT], BF16)
        BPL = BD
        BP = BK
        BA = BQ
        vp = [sb("vp%d" % l, [128, 4, 16 + TT], F32) for l in range(L)]
        Gb = [sb("Gb%d" % i, [128, 2 + TT], F32) for i in range(2)]
        CV = [sb("CV%d" % i, [128, 16 + TT], F32) for i in range(2)]
        P1, P2 = CV[0], CV[1]
        Gh = [sb("Gh%d" % l, [128, NFC, 2], F32) for l in range(L)]
        rs = sb("rs", [128, TT], F32)
        S = [sb("S%d" % l, [128, NH, 128], F32) for l in range(L)]
        Sdall = [sb("Sdall%d" % i, [128, 4, 128], BF16) for i in range(NH)]
        AT = [sb("AT%d" % i, [128, NH, 128], BF16) for i in range(2)]
        ex = sb("ex", [128, 2, NH, TT // 64], F32)
        identf = sb("identf", [128, 128], F32)
        identb = sb("identb", [128, 128], BF16)
        onesD = sb("onesD", [128, 128], BF16)
        onesV = sb("onesV", [128, 128], BF16)
        trim = sb("trim", [128, 128], F32)
        scanm = sb("scanm", [128, TT], F32)
        invc = sb("invc", [128, 4, TT], F32)
        epst = sb("epst", [128, 1], F32)
        tinyt = sb("tinyt", [128, 1], F32)
        invw = sb("invw", [128, 4], F32)
        cst = [sb("cst%d" % l, [128, NCST], F32) for l in range(L)]
        pproj = [sb("pproj%d" % l, [128, 4, 128], BF16) for l in range(L)]
        SGB = sb("SGB", [128, NDC, TT], BF16)
        SGA = sqz
        onesVf = sb("onesVf", [128, 128], F32)
        ring = [sb("ring%d" % i, [128, PIECE], BF16) for i in range(NSLOT)]

        psf = [st.enter_context(nc.psum_tensor("psf%d" % i, [128, 512], F32)) for i in range(7)]
        psb = st.enter_context(nc.psum_tensor("psb", [128, 1024], BF16))

        bank_ctr = [0]

        def bank(n=7, base=0):
            i = base + bank_ctr[0] % n
            bank_ctr[0] += 1
            return i

        T.dma("sp", identf[:], dr["ident"], writes=["identf"], key="c0")
        T.dma("sp", trim[:], dr["trimask"], writes=["trim"], key="c1")
        T.dma("sp", scanm[:], dr["scanmask"], writes=["scanm"], key="c2")
        T.dma("sp", invc[:, :, :T0], dr["invc1"], writes=["invc"], key="c3")
        for l in range(L):
            T.dma("sp", cst[l][:], dr["cst"][l], writes=[("cst", l)], key=("c4", l))
            T.dma("pool", pproj[l][:], dr["pproj"][l].rearrange("g c d -> c g d"), writes=[("pproj", l)], key=("c5", l))
        T.op("dve", lambda e: e.tensor_copy(out=identb[:], in_=identf[:]), reads=["identf"], writes=["identb"])
        T.op("pool", lambda e: e.memset(onesD[:], 1.0 / D), writes=["onesD"])
        T.op("pool", lambda e: e.memset(onesV[:], 1.0 / 128), writes=["onesV"])
        T.op("pool", lambda e: e.memset(onesVf[:], 1.0 / 128), writes=["onesVf"])
        T.op("pool", lambda e: e.memset(epst[:], EPS), writes=["epst"])
        T.op("pool", lambda e: e.memset(tinyt[:], 1e-30), writes=["tinyt"])
        for g in range(4):
            T.op("pool", lambda e, g=g: e.memset(invw[:, g:g + 1], 1.0 / 2 ** (g + 1)), writes=["invw"])
        T.op("pool", lambda e: e.memset(P1[:], 0.0), writes=[("CV", 0)])
        T.op("pool", lambda e: e.memset(P2[:], 0.0), writes=[("CV", 1)])
        for l in range(L):
            T.op("pool", lambda e, l=l: e.memset(S[l][:], 0.0), writes=[("S", l, hd) for hd in range(NH)])
            T.op("pool", lambda e, l=l: e.memset(vp[l][:], 0.0), writes=[("vp", l)])
            T.op("pool", lambda e, l=l: e.memset(Gh[l][:], 0.0), writes=[("Gh", l)])
        T.op("dve", lambda e: e.memset(cst[0][:, C_LB:C_LB + 4], 0.0), reads=[("cst", 0)], writes=[("cst", 0)])
        if L > 1:
            T.op("dve", lambda e: e.tensor_tensor(out=cst[1][:, C_LB:C_LB + 4], in0=cst[1][:, C_R1:C_R1 + 4],
                                                  in1=cst[1][:, C_R0:C_R0 + 4], op=ALU.subtract),
                 reads=[("cst", 1)], writes=[("cst", 1)])
            T.op("act", lambda e: e.activation(out=cst[1][:, C_LB:C_LB + 4], in_=cst[1][:, C_LB:C_LB + 4], func=AF.Sigmoid),
                 reads=[("cst", 1)], writes=[("cst", 1)])
        for l in range(L):
            T.op("dve", lambda e, l=l: e.tensor_scalar(out=cst[l][:, C_OML:C_OML + 4], in0=cst[l][:, C_LB:C_LB + 4],
                                                       scalar1=-1.0, scalar2=1.0, op0=ALU.mult, op1=ALU.add),
                 reads=[("cst", l)], writes=[("cst", l)])
            T.op("dve", lambda e, l=l: e.tensor_scalar(out=cst[l][:, C_NOML:C_NOML + 4], in0=cst[l][:, C_LB:C_LB + 4],
                                                       scalar1=-1.0, scalar2=None, op0=ALU.add),
                 reads=[("cst", l)], writes=[("cst", l)])

        def kview(w2d):
            return w2d.rearrange("(kc p) n -> p kc n", p=128)

        def piece_segments(l, i):
            win = kview(dr["w_in"][l])
            if i < 5:
                return [(win[:, :, i * 512:(i + 1) * 512], 0, 8, 512)]
            if i == 5 or i == 8:
                j = 0 if i == 5 else 1
                return [(kview(dr["w_bh"][l])[:, :, j * 512:(j + 1) * 512], 0, 4, 512),
                        (kview(dr["w_bp"][l])[:, :, j * 512:(j + 1) * 512], 2048, 4, 512)]
            if i in (6, 7, 9, 10):
                j = {6: 0, 7: 1, 9: 2, 10: 3}[i]
                return [(win[:, :, 2560 + j * 256:2560 + (j + 1) * 256], 0, 8, 256),
                        (win[:, :, 3584 + j * 256:3584 + (j + 1) * 256], 2048, 8, 256)]
            if i in (11, 12):
                j = i - 11
                return [(kview(dr["w_out"][l])[:, :, j * 512:(j + 1) * 512], 0, 8, 512)]
            if i < 24:
                j = i - 13
                return [(kview(dr["w_gate"][l])[:, :, j * 256:(j + 1) * 256], 0, 8, 256),
                        (kview(dr["w_up"][l])[:, :, j * 256:(j + 1) * 256], 2048, 8, 256)]
            dc = i - 24
            return [(kview(dr["w_down"][l])[:, :, dc * 128:(dc + 1) * 128], 0, NFC, 128)]

        nq = [0]
        NPS = 76

        def cast_piece(l, i):
            for k, (src, off, kc, n) in enumerate(piece_segments(l, i)):
                dst = wscr[l, i, :, off:off + kc * n].rearrange("p (kc n) -> p kc n", n=n)
                T.dma("pool", dst, src, writes=[("prosem", nq[0] % NPS), ("scr" if k == 0 else "scr2", l, i)], key=("pro", nq[0] % NPS))
                nq[0] += 1

        deferred = []
        for l in range(nlayers):
            for i in range(NPIECE):
                if nlayers > 1 and l == nlayers - 1 and i not in (1, 2) and k1x >= 3:
                    deferred.append((l, i))
                else:
                    cast_piece(l, i)

        slot_ctr = [0]

        def load_piece(l, i):
            s = slot_ctr[0] % NSLOT
            slot_ctr[0] += 1
            n = PIECE if i < 24 else NFC * 128
            T.dma("sp", ring[s][:, :n], wscr[l, i, :, :n], reads=[("scr", l, i), ("scr2", l, i)], writes=[("ring", s)], key=("ring", s))
            return s

        def mm_group(ps_ap, pairs, reads, writes, each=None, skip=False, first=True, last=True):
            n = len(pairs)
            for k, (lh, rh) in enumerate(pairs):
                st_, sp_ = (k == 0 and first), (k == n - 1 and last)
                rr = list(reads) + (list(each[k]) if each is not None else [])
                if skip:
                    fn = lambda e, lh=lh, rh=rh, st_=st_, sp_=sp_: e.matmul(ps_ap, lhsT=lh, rhs=rh, start=st_, stop=sp_, skip_group_check=True)
                else:
                    fn = lambda e, lh=lh, rh=rh, st_=st_, sp_=sp_: e.matmul(ps_ap, lhsT=lh, rhs=rh, start=st_, stop=sp_)
                T.op("pe", fn, reads=rr, writes=writes, signal=(k == n - 1))

        def stats_from_squares(sqbuf, sqname, nchunks, T_, ones, onesname, out_ap, out_res):
            b = bank()
            mm_group(psf[b][:, :T_], [(ones[:], sqbuf[:, c, :T_]) for c in range(nchunks)], reads=[onesname], writes=[("psf", b)],
                     each=[[(sqname, c)] for c in range(nchunks)])
            T.op("act", lambda e: e.activation(out=out_ap, in_=psf[b][:, :T_], func=AF.Ln, bias=epst[:, 0:1]),
                 reads=[("psf", b), "epst"], writes=[out_res])
            T.op("act", lambda e: e.activation(out=out_ap, in_=out_ap, func=AF.Exp, scale=-0.5),
                 reads=[out_res], writes=[out_res])

        def prenorm(l, T_, gcol):
            for c in range(NDC):
                if c % 2 == 0:
                    T.op("act", lambda e, c=c: e.activation(out=sqz[:, c, :T_], in_=h[:, c, :T_], func=AF.Square),
                         reads=[("h", c)], writes=[("sqz", c)])
                else:
                    T.op("dve", lambda e, c=c: e.tensor_tensor(out=sqz[:, c, :T_], in0=h[:, c, :T_], in1=h[:, c, :T_], op=ALU.mult),
                         reads=[("h", c)], writes=[("sqz", c)])
            stats_from_squares(sqz, "sqz", NDC, T_, onesD, "onesD", rs[:, :T_], "rs")
            for c in range(NDC):
                T.op("dve", lambda e, c=c: e.scalar_tensor_tensor(out=u[:, c, :T_], in0=h[:, c, :T_],
                                                                 scalar=cst[l][:, gcol + c:gcol + c + 1], in1=rs[:, :T_],
                                                                 op0=ALU.mult, op1=ALU.mult),
                     reads=[("h", c), "rs", ("cst", l)], writes=[("u", c)])

        def y32(dc):
            return (FS if dc < 4 else FK)[:, dc % 4, :]

        def y32res(dc):
            return ("FS" if dc < 4 else "FK", dc % 4)

        def evac_y(dc, b, T_, sqbuf, sqname):
            T.op("dve", lambda e: e.tensor_copy(out=y32(dc)[:, :T_], in_=psf[b][:, :T_]), reads=[("psf", b)], writes=[y32res(dc)])
            T.op("pool", lambda e: e.tensor_tensor(out=sqbuf[:, dc, :T_], in0=y32(dc)[:, :T_], in1=y32(dc)[:, :T_], op=ALU.mult),
                 reads=[y32res(dc)], writes=[(sqname, dc)])

        def postnorm(l, T_, gcol, sqbuf, sqname):
            stats_from_squares(sqbuf, sqname, NDC, T_, onesD, "onesD", rs[:, :T_], "rs")
            for c in range(NDC):
                T.op("dve", lambda e, c=c: e.scalar_tensor_tensor(out=y32(c)[:, :T_], in0=y32(c)[:, :T_],
                                                                   scalar=cst[l][:, gcol + c:gcol + c + 1], in1=rs[:, :T_],
                                                                   op0=ALU.mult, op1=ALU.mult),
                     reads=["rs", ("cst", l)], writes=[y32res(c)])
                T.op("pool" if c % 2 == 0 else "dve", lambda e, c=c: e.tensor_tensor(out=h[:, c, :T_], in0=h[:, c, :T_], in1=y32(c)[:, :T_], op=ALU.add),
                     reads=[y32res(c)], writes=[("h", c)])

        def proj_fm(l, piece, T_, evac):
            s = load_piece(l, piece)
            W = ring[s]
            for hd in range(4):
                b = bank()
                mm_group(psf[b][:, :T_], [(W[:, kc * 512 + hd * 128:kc * 512 + (hd + 1) * 128], u[:, kc, :T_]) for kc in range(NDC)],
                         reads=[("ring", s)], writes=[("psf", b)], each=[[("u", kc)] for kc in range(NDC)])
                evac(hd, b)

        def mixer(l, T_, tile0, state_only=False, mid_hook=None):
            nblk = T_ // 128
            nch = T_ // 64
            c_ = cst[l]
            prenorm(l, T_, C_MPRE)
            if not state_only:
                proj_fm(l, 0, T_, lambda hd, b: T.op("act", lambda e: e.copy(out=FQ[:, hd, :T_], in_=psf[b][:, :T_]),
                                                     reads=[("psf", b)], writes=[("FQ", hd)]))
            proj_fm(l, 1, T_, lambda hd, b: T.op("act", lambda e: e.activation(out=FS[:, hd, :T_], in_=psf[b][:, :T_], func=AF.Sigmoid),
                                                 reads=[("psf", b)], writes=[("FS", hd)]))

            if mid_hook is not None:
                mid_hook()
            def hv(buf, hd):
                return buf[:, hd, :T_].rearrange("p (c j) -> p c j", j=64)

            def st_kk(hd):
                T.op("dve", lambda e: e.tensor_scalar(out=FK[:, hd, :T_], in0=FS[:, hd, :T_], scalar1=c_[:, C_NOML + hd:C_NOML + hd + 1],
                                                      scalar2=c_[:, C_OML + hd:C_OML + hd + 1], op0=ALU.mult, op1=ALU.add),
                     reads=[("FS", hd), ("cst", l)], writes=[("FK", hd)])

            def st_clamp(hd):
                T.op("dve", lambda e: e.tensor_scalar(out=FS[:, hd, :T_], in0=FS[:, hd, :T_], scalar1=tinyt[:, 0:1],
                                                      scalar2=c_[:, C_OML + hd:C_OML + hd + 1], op0=ALU.max, op1=ALU.mult),
                     reads=[("FS", hd), ("cst", l), "tinyt"], writes=[("FS", hd)])

            def st_ln(hd):
                T.op("act", lambda e: e.activation(out=FS[:, hd, :T_], in_=FS[:, hd, :T_], func=AF.Ln, bias=c_[:, C_LB + hd:C_LB + hd + 1]),
                     reads=[("FS", hd), ("cst", l)], writes=[("FS", hd)])

            def st_scan(hd):
                T.op("dve", lambda e: e.tensor_tensor_scan(out=FB[:, hd, :T_], data0=scanm[:, :T_], data1=FS[:, hd, :T_],
                                                           initial=0.0, op0=ALU.mult, op1=ALU.add),
                     reads=[("FS", hd), "scanm"], writes=[("FB", hd)])

            def st_ex(hd):
                FBh = hv(FB, hd)
                T.op("act", lambda e: e.activation(out=ex[:, 0, hd, :nch], in_=FBh[:, :, 31], func=AF.Exp), reads=[("FB", hd)], writes=[("ex0", hd)])
                T.op("act", lambda e: e.activation(out=ex[:, 1, hd, :nch], in_=FBh[:, :, 63], func=AF.Exp), reads=[("FB", hd)], writes=[("ex1", hd)])

            def st_d(hd):
                FBh, FSh = hv(FB, hd), hv(FS, hd)
                bmid = FBh[:, :, 31:32].broadcast_to([128, nch, 64])
                T.op("dve", lambda e: e.tensor_tensor(out=FSh, in0=FBh, in1=bmid, op=ALU.subtract), reads=[("FB", hd)], writes=[("FS", hd)])

            def st_e1(hd):
                T.op("act", lambda e: e.activation(out=FE[:, hd, :T_], in_=FS[:, hd, :T_], func=AF.Exp), reads=[("FS", hd)], writes=[("FE", hd)])

            def st_bq(hd):
                T.op("pool", lambda e: e.tensor_tensor(out=BQ[:, hd, :T_], in0=FQ[:, hd, :T_], in1=FE[:, hd, :T_], op=ALU.mult),
                     reads=[("FQ", hd), ("FE", hd)], writes=[("BQ", hd)])

            def st_e2(hd):
                T.op("act", lambda e: e.activation(out=FQ[:, hd, :T_], in_=FS[:, hd, :T_], func=AF.Exp, scale=-1.0), reads=[("FS", hd)], writes=[("FQ", hd)])

            def st_bk(hd):
                T.op("dve", lambda e: e.tensor_tensor(out=BK[:, hd, :T_], in0=FK[:, hd, :T_], in1=FQ[:, hd, :T_], op=ALU.mult),
                     reads=[("FK", hd), ("FQ", hd)], writes=[("BK", hd)])

            def st_d2(hd):
                FBh, FSh = hv(FB, hd), hv(FS, hd)
                blast = FBh[:, :, 63:64].broadcast_to([128, nch, 64])
                T.op("dve", lambda e: e.tensor_tensor(out=FSh, in0=blast, in1=FBh, op=ALU.subtract), reads=[("FB", hd)], writes=[("FS", hd)])

            def st_e3(hd):
                T.op("act", lambda e: e.activation(out=FE[:, hd, :T_], in_=FS[:, hd, :T_], func=AF.Exp), reads=[("FS", hd)], writes=[("FE", hd)])

            def st_bd(hd):
                T.op("pool", lambda e: e.tensor_tensor(out=BD[:, hd, :T_], in0=FK[:, hd, :T_], in1=FE[:, hd, :T_], op=ALU.mult),
                     reads=[("FK", hd), ("FE", hd)], writes=[("BD", hd)])

            def so_scan(hd):
                T.op("dve", lambda e: e.tensor_tensor_scan(out=FQ[:, hd, :T_], data0=scanm[:, :T_], data1=FS[:, hd, :T_],
                                                           initial=0.0, op0=ALU.mult, op1=ALU.add),
                     reads=[("FS", hd), "scanm"], writes=[("FQ", hd)])

            def so_ex(hd):
                Fh = hv(FQ, hd)
                T.op("act", lambda e: e.activation(out=ex[:, 1, hd, :nch], in_=Fh[:, :, 63], func=AF.Exp), reads=[("FQ", hd)], writes=[("ex1", hd)])

            def so_d2(hd):
                Fh, FSh = hv(FQ, hd), hv(FS, hd)
                blast = Fh[:, :, 63:64].broadcast_to([128, nch, 64])
                T.op("dve", lambda e: e.tensor_tensor(out=FSh, in0=blast, in1=Fh, op=ALU.subtract), reads=[("FQ", hd)], writes=[("FS", hd)])

            def so_e3(hd):
                T.op("act", lambda e: e.activation(out=FS[:, hd, :T_], in_=FS[:, hd, :T_], func=AF.Exp), reads=[("FS", hd)], writes=[("FS", hd)])

            def so_bd(hd):
                T.op("pool", lambda e: e.tensor_tensor(out=BD[:, hd, :T_], in0=FK[:, hd, :T_], in1=FS[:, hd, :T_], op=ALU.mult),
                     reads=[("FK", hd), ("FS", hd)], writes=[("BD", hd)])

            stages = [st_kk, st_clamp, st_ln, st_scan, st_ex, st_d, st_e1, st_bq, st_e2, st_bk, st_d2, st_e3, st_bd]

            def set_v():
                s = load_piece(l, 2)
                W = ring[s]
                for blk in range(nblk):
                    b = bank()
                    mm_group(psf[b][:, :512], [(u[:, kc, blk * 128:(blk + 1) * 128], W[:, kc * 512:(kc + 1) * 512]) for kc in range(NDC)],
                             reads=[("ring", s)], writes=[("psf", b)], each=[[("u", kc)] for kc in range(NDC)])
                    T.op("act", lambda e, blk=blk, b=b: e.copy(out=BV[:, blk, :], in_=psf[b][:, :512]),
                         reads=[("psf", b)], writes=[("BV", blk)])

            def set_g():
                proj_fm(l, 3, T_, lambda hd, b: T.op("act", lambda e: e.activation(out=BG[:, hd, :T_], in_=psf[b][:, :T_], func=AF.Sigmoid),
                                                     reads=[("psf", b)], writes=[("BG", hd)]))

            def set_vp():
                proj_fm(l, 4, T_, lambda hd, b: T.op("act", lambda e: e.copy(out=vp[l][:, hd, 16:16 + T_], in_=psf[b][:, :T_]),
                                                     reads=[("psf", b)], writes=[("vp", l)]))

            def set_gate(j4, pg):
                def f():
                    sg_ = load_piece(l, pg)
                    Wg = ring[sg_]
                    for sub in range(2):
                        dc = 2 * j4 + sub
                        co = sub * 128
                        bga, bgb = bank(), bank()
                        eu = [[("u", kc)] for kc in range(NDC)]
                        mm_group(psf[bga][:, :T_], [(Wg[:, kc * 256 + co:kc * 256 + co + 128], u[:, kc, :T_]) for kc in range(NDC)],
                                 reads=[("ring", sg_)], writes=[("psf", bga)], each=eu)
                        mm_group(psf[bgb][:, :T_], [(Wg[:, 2048 + kc * 256 + co:2048 + kc * 256 + co + 128], u[:, kc, :T_]) for kc in range(NDC)],
                                 reads=[("ring", sg_)], writes=[("psf", bgb)], each=eu)
                        T.op("act", lambda e, dc=dc, bga=bga: e.activation(out=SGA[:, dc, :T_], in_=psf[bga][:, :T_], func=AF.Sigmoid),
                             reads=[("psf", bga)], writes=[("sqz", dc)])
                        T.op("act", lambda e, dc=dc, bgb=bgb: e.activation(out=SGB[:, dc, :T_], in_=psf[bgb][:, :T_], func=AF.Sigmoid),
                             reads=[("psf", bgb)], writes=[("SGB", dc)])
                return f

            def set_pool():
                for g in range(4):
                    X = vp[l][:, g, :]
                    LL = 16 + T_
                    src = X
                    bufs = [P1, P2]
                    sh = 1
                    for k in range(g + 1):
                        dst = bufs[k % 2]
                        T.op("pool", lambda e, src=src, dst=dst, sh=sh: e.tensor_tensor(out=dst[:, sh:LL], in0=src[:, sh:LL], in1=src[:, 0:LL - sh], op=ALU.add),
                             reads=[("vp", l), ("CV", 0), ("CV", 1)], writes=[("CV", k % 2)])
                        src = dst
                        sh *= 2
                    R = src
                    if tile0:
                        in1 = invc[:, g, :T_]
                    else:
                        in1 = invw[:, g:g + 1].broadcast_to([128, T_])
                    T.op("pool", lambda e, R=R, in1=in1: e.tensor_tensor(out=R[:, 16:16 + T_], in0=R[:, 16:16 + T_], in1=in1, op=ALU.mult),
                         reads=[("CV", 0), ("CV", 1), "invc", "invw"], writes=[("CV", 0), ("CV", 1)])
                    T.op("pool", lambda e, R=R, g=g, X=X: e.tensor_tensor(out=BPL[:, g, :T_], in0=R[:, 16:16 + T_], in1=X[:, 16:16 + T_], op=ALU.subtract),
                         reads=[("CV", 0), ("CV", 1), ("vp", l)], writes=[("BD", g)])
                T.op("pool", lambda e: e.tensor_copy(out=vp[l][:, :, 0:16], in_=vp[l][:, :, T_:T_ + 16]), reads=[("vp", l)], writes=[("vp", l)])

            def set_tr():
                for blk in range(nblk):
                    half = blk % 2
                    for hd in range(NH):
                        T.op("pe", lambda e, blk=blk, hd=hd, half=half: e.transpose(out=psb[:, half * 512 + hd * 128:half * 512 + (hd + 1) * 128],
                                                                                    in_=BD[:, hd, blk * 128:(blk + 1) * 128], identity=identb[:]),
                             reads=[("BD", hd), "identb"], writes=["psb"], signal=(hd == NH - 1))
                    T.op("act", lambda e, blk=blk, half=half: e.copy(out=BKD[:, blk, :], in_=psb[:, half * 512:(half + 1) * 512]),
                         reads=["psb"], writes=[("BKD", blk)])

            pe_sets = [set_v, set_g, set_vp, set_gate(0, 6), set_gate(1, 7), set_gate(2, 9), set_gate(3, 10)]
            if state_only:
                stages = [st_kk, st_clamp, st_ln, so_scan, so_ex, so_d2, so_e3, so_bd]
                pe_sets = [set_v]
            for si, stg in enumerate(stages):
                for hd in range(NH):
                    stg(hd)
                if si % 2 == 0 and pe_sets:
                    pe_sets.pop(0)()
            while pe_sets:
                pe_sets.pop(0)()
            if state_only:
                def part2():
                    set_tr()
                    for pr in range(nblk):
                        bU0, bU1 = bank(), bank()
                        for j, bU in ((0, bU0), (1, bU1)):
                            for hd in range(NH):
                                T.op("pe", lambda e, hd=hd, j=j, bU=bU, pr=pr: e.matmul(psf[bU][:, hd * 128:(hd + 1) * 128],
                                                                                        lhsT=BKD[j * 64:(j + 1) * 64, pr, hd * 128:(hd + 1) * 128],
                                                                                        rhs=BV[j * 64:(j + 1) * 64, pr, hd * 128:(hd + 1) * 128], start=True, stop=True),
                                     reads=[("BKD", pr), ("BV", pr)], writes=[("psf", bU)], signal=(hd == NH - 1))
                        for j, bU in ((0, bU0), (1, bU1)):
                            c = 2 * pr + j
                            for hd in range(NH):
                                T.op("dve", lambda e, hd=hd, c=c, bU=bU: e.scalar_tensor_tensor(out=S[l][:, hd, :], in0=S[l][:, hd, :], scalar=ex[:, 1, hd, c:c + 1],
                                                                                              in1=psf[bU][:, hd * 128:(hd + 1) * 128], op0=ALU.mult, op1=ALU.add),
                                     reads=[("psf", bU), ("ex1", hd), ("S", l, hd)], writes=[("S", l, hd)])
                return part2
            set_tr()
            set_pool()

            if stop < 5:
                return
            trim_bc = trim[:, None, :].broadcast_to([128, NH, 128]) if False else None
            pend = {}

            def SU(pr):
                bS, bU0, bU1 = bank(), bank(), bank()
                for hd in range(NH):
                    T.op("pe", lambda e, hd=hd: e.matmul(psf[bS][:, hd * 128:(hd + 1) * 128], lhsT=BK[:, hd, pr * 128:(pr + 1) * 128],
                                                         rhs=BQ[:, hd, pr * 128:(pr + 1) * 128], start=True, stop=True),
                         reads=[("BK", hd), ("BQ", hd)], writes=[("psf", bS)], signal=(hd == NH - 1))
                for j, bU in ((0, bU0), (1, bU1)):
                    for hd in range(NH):
                        T.op("pe", lambda e, hd=hd, j=j, bU=bU: e.matmul(psf[bU][:, hd * 128:(hd + 1) * 128],
                                                                         lhsT=BKD[j * 64:(j + 1) * 64, pr, hd * 128:(hd + 1) * 128],
                                                                         rhs=BV[j * 64:(j + 1) * 64, pr, hd * 128:(hd + 1) * 128], start=True, stop=True),
                             reads=[("BKD", pr), ("BV", pr)], writes=[("psf", bU)], signal=(hd == NH - 1))
                ai = pr % 2
                for hd in range(NH):
                    T.op("dve", lambda e, hd=hd: e.tensor_tensor(out=AT[ai][:, hd, :], in0=psf[bS][:, hd * 128:(hd + 1) * 128], in1=trim[:], op=ALU.mult),
                         reads=[("psf", bS), "trim"], writes=[("AT", ai, hd)])
                for j, bU in ((0, bU0), (1, bU1)):
                    c = 2 * pr + j
                    for hd in range(NH):
                        T.op("dve", lambda e, hd=hd, c=c: e.tensor_scalar(out=Sdall[hd][:, c % 4, :], in0=S[l][:, hd, :], scalar1=ex[:, 0, hd, c:c + 1],
                                                                          scalar2=None, op0=ALU.mult),
                             reads=[("S", l, hd), ("ex0", hd)], writes=[("Sd", hd, c % 4)])
                    for hd in range(NH):
                        T.op("dve", lambda e, hd=hd, c=c, bU=bU: e.scalar_tensor_tensor(out=S[l][:, hd, :], in0=S[l][:, hd, :], scalar=ex[:, 1, hd, c:c + 1],
                                                                                      in1=psf[bU][:, hd * 128:(hd + 1) * 128], op0=ALU.mult, op1=ALU.add),
                             reads=[("psf", bU), ("ex1", hd), ("S", l, hd)], writes=[("S", l, hd)])

            def O(pr):
                bO = bank()
                ai = pr % 2
                k = 0
                for hd in range(NH):
                    for j in range(2):
                        c = 2 * pr + j
                        T.op("pe", lambda e, hd=hd, c=c, j=j, k=k: e.matmul(psf[bO][:, hd * 128 + j * 64:hd * 128 + (j + 1) * 64], lhsT=Sdall[hd][:, c % 4, :],
                                                                           rhs=BQ[:, hd, c * 64:(c + 1) * 64], start=(k == 0), stop=False, skip_group_check=True),
                             reads=[("Sd", hd, c % 4), ("BQ", hd)], writes=[("psf", bO)], signal=False)
                        k += 1
                    T.op("pe", lambda e, hd=hd: e.matmul(psf[bO][:, hd * 128:(hd + 1) * 128], lhsT=BV[:, pr, hd * 128:(hd + 1) * 128], rhs=AT[ai][:, hd, :],
                                                         start=False, stop=(hd == NH - 1), skip_group_check=True),
                         reads=[("BV", pr), ("AT", ai, hd)], writes=[("psf", bO)], signal=(hd == NH - 1))
                pv = psf[bO][:, :512].rearrange("p (h t) -> p h t", t=128)
                T.op("act", lambda e: e.copy(out=FQ[:, :, pr * 128:(pr + 1) * 128], in_=pv), reads=[("psf", bO)], writes=[("FQ", hd) for hd in range(NH)])
                T.op("act", lambda e: e.activation(out=FE[:, :, pr * 128:(pr + 1) * 128], in_=pv, func=AF.Square),
                     reads=[("psf", bO)], writes=[("FE", hd) for hd in range(NH)])

            SU(0)
            for pr in range(nblk):
                if pr + 1 < nblk:
                    SU(pr + 1)
                O(pr)
            if stop < 6:
                return
            allF = lambda n: [(n, hd) for hd in range(NH)]
            for hd in range(NH):
                b = bank()
                mm_group(psf[b][:, :T_], [(onesVf[:], FE[:, hd, :T_])], reads=[("FE", hd), "onesVf"], writes=[("psf", b)])
                T.op("act", lambda e, hd=hd, b=b: e.activation(out=FS[:, hd, :T_], in_=psf[b][:, :T_], func=AF.Ln, bias=epst[:, 0:1]),
                     reads=[("psf", b), "epst"], writes=[("FS", hd)])
                T.op("act", lambda e, hd=hd: e.activation(out=FS[:, hd, :T_], in_=FS[:, hd, :T_], func=AF.Exp, scale=-0.5),
                     reads=[("FS", hd)], writes=[("FS", hd)])
                T.op("pool", lambda e, hd=hd: e.tensor_tensor(out=FQ[:, hd, :T_], in0=FQ[:, hd, :T_], in1=FS[:, hd, :T_], op=ALU.mult),
                     reads=[("FS", hd), ("FQ", hd)], writes=[("FQ", hd)])
                T.op("dve", lambda e, hd=hd: e.scalar_tensor_tensor(out=BA[:, hd, :T_], in0=FQ[:, hd, :T_], scalar=c_[:, C_OG + hd:C_OG + hd + 1],
                                                                    in1=BG[:, hd, :T_], op0=ALU.mult, op1=ALU.mult),
                     reads=[("FQ", hd), ("BG", hd), ("cst", l)], writes=[("BQ", hd)])
            if stop < 7:
                return
            for g in range(4):
                b = bank()
                mm_group(psf[b][:, :T_], [(pproj[l][:, g, :], BPL[:, g, :T_])], reads=[("pproj", l), ("BD", g)], writes=[("psf", b)])
                T.op("act", lambda e, g=g, b=b: e.mul(out=BP[:, g, :T_], in_=psf[b][:, :T_], mul=c_[:, C_PS + g:C_PS + g + 1]),
                     reads=[("psf", b), ("cst", l)], writes=[("BK", g)])

            if stop < 8:
                return
            for half in range(2):
                sb_ = load_piece(l, 5 if half == 0 else 8)
                Wb = ring[sb_]
                for q4 in range(4):
                    dc = half * 4 + q4
                    cb = q4 * 128
                    bbh, bbp = bank(), bank()
                    mm_group(psf[bbh][:, :T_], [(Wb[:, kc * 512 + cb:kc * 512 + cb + 128], BA[:, kc, :T_]) for kc in range(4)],
                             reads=[("ring", sb_)], writes=[("psf", bbh)], each=[[("BQ", kc)] for kc in range(4)])
                    mm_group(psf[bbp][:, :T_], [(Wb[:, 2048 + kc * 512 + cb:2048 + kc * 512 + cb + 128], BP[:, kc, :T_]) for kc in range(4)],
                             reads=[("ring", sb_)], writes=[("psf", bbp)], each=[[("BK", kc)] for kc in range(4)])
                    za = FQ[:, dc % 2, :T_]
                    zb = FQ[:, 2 + dc % 2, :T_]
                    T.op("dve", lambda e, za=za, bbh=bbh, dc=dc: e.tensor_tensor(out=za, in0=SGA[:, dc, :T_], in1=psf[bbh][:, :T_], op=ALU.mult),
                         reads=[("psf", bbh), ("sqz", dc)], writes=[("FQ", dc % 2)])
                    T.op("dve", lambda e, zb=zb, bbp=bbp, dc=dc: e.tensor_tensor(out=zb, in0=SGB[:, dc, :T_], in1=psf[bbp][:, :T_], op=ALU.mult),
                         reads=[("psf", bbp), ("SGB", dc)], writes=[("FQ", 2 + dc % 2)])
                    T.op("pool", lambda e, za=za, zb=zb, dc=dc: e.tensor_tensor(out=sqz[:, dc, :T_], in0=za, in1=zb, op=ALU.add),
                         reads=[("FQ", dc % 2), ("FQ", 2 + dc % 2)], writes=[("sqz", dc)])
            if stop < 9:
                return
            for half in range(2):
                s = load_piece(l, 11 + half)
                W = ring[s]
                for q4 in range(4):
                    dc = half * 4 + q4
                    b = bank()
                    mm_group(psf[b][:, :T_], [(W[:, kc * 512 + q4 * 128:kc * 512 + (q4 + 1) * 128], sqz[:, kc, :T_]) for kc in range(NDC)],
                             reads=[("ring", s)], writes=[("psf", b)], each=[[("sqz", kc)] for kc in range(NDC)])
                    evac_y(dc, b, T_, u, "u")
            postnorm(l, T_, C_MPOST, u, "u")

        def ffn(l, T_):
            if stop < 10:
                return
            c_ = cst[l]
            prenorm(l, T_, C_FPRE)
            eu = [[("u", kc)] for kc in range(NDC)]

            def mtile(fc):
                blkt = [BQ, BK, BD, BKD, BV, BG][fc // 4]
                return blkt[:, fc % 4, :T_], (["BQ", "BK", "BD", "BKD", "BV", "BG"][fc // 4], fc % 4)

            for j in range(11):
                s = load_piece(l, 13 + j)
                W = ring[s]
                for sub in range(2):
                    fc = 2 * j + sub
                    bg, bu = bank(), bank()
                    mm_group(psf[bg][:, :T_], [(W[:, kc * 256 + sub * 128:kc * 256 + (sub + 1) * 128], u[:, kc, :T_]) for kc in range(NDC)],
                             reads=[("ring", s)], writes=[("psf", bg)], each=eu)
                    mm_group(psf[bu][:, :T_], [(W[:, 2048 + kc * 256 + sub * 128:2048 + kc * 256 + (sub + 1) * 128], u[:, kc, :T_]) for kc in range(NDC)],
                             reads=[("ring", s)], writes=[("psf", bu)], each=eu)
                    gi = fc % 2
                    G = Gb[gi]
                    cv = CV[gi]
                    w0 = c_[:, C_CW + fc:C_CW + fc + 1]
                    w1 = c_[:, C_CW + NFC + fc:C_CW + NFC + fc + 1]
                    w2 = c_[:, C_CW + 2 * NFC + fc:C_CW + 2 * NFC + fc + 1]
                    bb_ = c_[:, C_CB + fc:C_CB + fc + 1]
                    T.op("pool", lambda e, G=G, fc=fc: e.tensor_copy(out=G[:, 0:2], in_=Gh[l][:, fc, :]), reads=[("Gh", l)], writes=[("Gbh", gi)])
                    T.op("act", lambda e, G=G, bg=bg: e.copy(out=G[:, 2:2 + T_], in_=psf[bg][:, :T_]), reads=[("psf", bg)], writes=[("Gb", gi)])
                    T.op("act", lambda e, cv=cv, bg=bg, w2=w2, bb_=bb_: e.activation(out=cv[:, :T_], in_=psf[bg][:, :T_], func=AF.Identity, scale=w2, bias=bb_),
                         reads=[("psf", bg), ("cst", l)], writes=[("CV", gi)])
                    T.op("pool", lambda e, G=G, fc=fc: e.tensor_copy(out=Gh[l][:, fc, :], in_=G[:, T_:T_ + 2]), reads=[("Gb", gi)], writes=[("Gh", l)])
                    T.op("dve", lambda e, G=G, cv=cv, w1=w1: e.scalar_tensor_tensor(out=cv[:, :T_], in0=G[:, 1:1 + T_], scalar=w1, in1=cv[:, :T_],
                                                                                  op0=ALU.mult, op1=ALU.add),
                         reads=[("Gb", gi), ("Gbh", gi), ("CV", gi), ("cst", l)], writes=[("CV", gi)])
                    T.op("dve", lambda e, G=G, cv=cv, w0=w0: e.scalar_tensor_tensor(out=cv[:, :T_], in0=G[:, 0:T_], scalar=w0, in1=cv[:, :T_],
                                                                                  op0=ALU.mult, op1=ALU.add),
                         reads=[("Gb", gi), ("Gbh", gi), ("CV", gi), ("cst", l)], writes=[("CV", gi)])
                    T.op("act", lambda e, cv=cv: e.activation(out=cv[:, :T_], in_=cv[:, :T_], func=AF.Gelu_apprx_tanh),
                         reads=[("CV", gi)], writes=[("CV", gi)])
                    mt, mres = mtile(fc)
                    T.op("dve", lambda e, cv=cv, mt=mt, bu=bu: e.tensor_tensor(out=mt, in0=cv[:, :T_], in1=psf[bu][:, :T_], op=ALU.mult),
                         reads=[("CV", gi), ("psf", bu)], writes=[mres])
            for dc in range(NDC):
                s = load_piece(l, 24 + dc)
                W = ring[s]
                b = bank()
                pairs = []
                each = []
                for fc in range(NFC):
                    mt, mres = mtile(fc)
                    pairs.append((W[:, fc * 128:(fc + 1) * 128], mt))
                    each.append([mres])
                mm_group(psf[b][:, :T_], pairs, reads=[("ring", s)], writes=[("psf", b)], each=each)
                evac_y(dc, b, T_, sqz, "sqz")
            postnorm(l, T_, C_FPOST, sqz, "sqz")

        def tokhalf(hf, store=False):
            if store:
                return (FS if hf == 0 else FK), ("FS" if hf == 0 else "FK")
            return (FB if hf == 0 else FE), ("FB" if hf == 0 else "FE")

        def load_dma(src_rows, T_):
            nblk = T_ // 128
            srcv = src_rows.rearrange("(b p) (hf f) -> p b hf f", p=128, hf=2)
            for hf in range(2):
                tb, tn = tokhalf(hf)
                T.dma("sp", tb[:, :nblk, :], srcv[:, :, hf, :], writes=[(tn, k) for k in range(4)], key=("in", hf))

        def load_tile(T_):
            nblk = T_ // 128
            for dc in range(NDC):
                tb, tn = tokhalf(dc // 4)
                b = bank()
                for blk in range(nblk):
                    T.op("pe", lambda e, tb=tb, dc=dc, blk=blk, b=b: e.transpose(out=psf[b][:, blk * 128:(blk + 1) * 128],
                                                                                 in_=tb[:, blk, (dc % 4) * 128:(dc % 4 + 1) * 128], identity=identf[:]),
                         reads=[(tn, blk), "identf"], writes=[("psf", b)], signal=(blk == nblk - 1))
                if dc % 2 == 0:
                    T.op("act", lambda e, dc=dc, b=b: e.copy(out=h[:, dc, :T_], in_=psf[b][:, :T_]), reads=[("psf", b)], writes=[("h", dc)])
                else:
                    T.op("dve", lambda e, dc=dc, b=b: e.tensor_copy(out=h[:, dc, :T_], in_=psf[b][:, :T_]), reads=[("psf", b)], writes=[("h", dc)])

        def store_tile(dst_rows, T_):
            nblk = T_ // 128
            dstv = dst_rows.rearrange("(b p) (hf f) -> p b hf f", p=128, hf=2)
            toks = []
            for hf in range(2):
                tb, tn = tokhalf(hf, store=True)
                for blk in range(nblk):
                    b = bank()
                    for q4 in range(4):
                        dc = hf * 4 + q4
                        T.op("pe", lambda e, dc=dc, blk=blk, b=b, q4=q4: e.transpose(out=psf[b][:, q4 * 128:(q4 + 1) * 128],
                                                                                   in_=h[:, dc, blk * 128:(blk + 1) * 128], identity=identf[:]),
                             reads=[("h", dc), "identf"], writes=[("psf", b)], signal=(q4 == 3))
                    if blk % 2 == 0:
                        T.op("act", lambda e, tb=tb, blk=blk, b=b: e.copy(out=tb[:, blk, :], in_=psf[b][:, :512]), reads=[("psf", b)], writes=[(tn, blk)])
                    else:
                        T.op("dve", lambda e, tb=tb, blk=blk, b=b: e.tensor_copy(out=tb[:, blk, :], in_=psf[b][:, :512]), reads=[("psf", b)], writes=[(tn, blk)])
                toks.append(T.dma("sp", dstv[:, :, hf, :], tb[:, :nblk, :], reads=[(tn, k) for k in range(4)], key=("out", hf)))
            return toks

        last = []
        seq1 = [("meta", None, T0)] + [("x1", t, TT) for t in range(k1x)]
        seq2 = [("x2", t, TT) for t in range(k2)]

        def src_of(item):
            nm, t, T_ = item
            return dr[nm] if t is None else dr[nm][t * TT:(t + 1) * TT, :]

        items = [(1, it) for it in seq1] + [(2, it) for it in seq2]
        load_dma(src_of(items[0][1]), items[0][1][2])
        first2 = True
        pending = [None]
        for idx, (ph, it) in enumerate(items):
            T_ = it[2]
            last_prefix = (ph == 1 and idx == len(seq1) - 1)
            is_meta = (ph == 1 and idx == 0) or last_prefix
            if last_prefix:
                T.dma("sp", invc[:], dr["invc2"], writes=["invc"], key="c3")
            load_tile(T_)
            if ph == 1 and idx >= 1 and deferred:
                nper = -(-30 // max(1, (k1x - 2)))
                for _ in range(min(nper, len(deferred))):
                    cast_piece(*deferred.pop(0))
            for l in range(nlayers):
                so = (ph == 1 and l == nlayers - 1 and not last_prefix and nlayers > 1)
                if so:
                    pending[0] = mixer(l, T_, is_meta, state_only=True)
                else:
                    hook, pending[0] = (pending[0], None) if l == 0 else (None, pending[0])
                    mixer(l, T_, is_meta, mid_hook=hook)
                if l == nlayers - 1 and idx + 1 < len(items):
                    load_dma(src_of(items[idx + 1][1]), items[idx + 1][1][2])
                if not so:
                    ffn(l, T_)
            if ph == 2:
                first2 = False
                last = store_tile(out[it[1] * TT:(it[1] + 1) * TT, :], T_)
        if pending[0] is not None:
            pending[0]()
        assert not deferred
        for tk in last:
            T.wait_tok("sp", tk)

        with nc.Block() as block:
            @block.sync
            def _(e):
                for th in T.thunks["sp"]:
                    th(e)

            @block.tensor
            def _(e):
                for th in T.thunks["pe"]:
                    th(e)

            @block.scalar
            def _(e):
                for th in T.thunks["act"]:
                    th(e)

            @block.vector
            def _(e):
                for th in T.thunks["dve"]:
                    th(e)

            @block.gpsimd
            def _(e):
                for th in T.thunks["pool"]:
                    th(e)
    return nc


def host_consts():
    ident = np.eye(128, dtype=np.float32)
    s = np.arange(128)[:, None]
    t = np.arange(128)[None, :]
    trimask = ((s // 64 == t // 64) & (s <= t)).astype(np.float32)
    scanmask = np.ones((128, TT), np.float32)
    scanmask[:, ::64] = 0.0

    def table(width, with_meta):
        tb = np.ones((128, 4, width), np.float32)
        for g, w in enumerate((2, 4, 8, 16)):
            if with_meta:
                pos = np.arange(width) - (width - NMETA) + 1
                cnt = np.minimum(np.maximum(pos, 1), w).astype(np.float32)
            else:
                cnt = np.full(width, w, np.float32)
            tb[:, g, :] = (1.0 / cnt)[None, :]
        return tb
    return ident, trimask, scanmask, table


def make_cst(inp):
    cst = np.zeros((L, 128, NCST), np.float32)
    lbr = np.asarray(inp["hgrn_lower_bounds"], np.float32)
    for l in range(L):
        cst[l, :, C_R0:C_R0 + 4] = lbr[0].reshape(4, 128).T
        cst[l, :, C_R1:C_R1 + 4] = lbr[1].reshape(4, 128).T
        cst[l, :, C_OG:C_OG + 4] = np.asarray(inp["hgrn_out_norm"][l]).reshape(4, 128).T
        cst[l, :, C_PS:C_PS + 4] = np.asarray(inp["pool_scale"][l]).reshape(4, 128).T
        cst[l, :, C_MPRE:C_MPRE + 8] = np.asarray(inp["mix_norm_pre"][l]).reshape(8, 128).T
        cst[l, :, C_MPOST:C_MPOST + 8] = np.asarray(inp["mix_norm_post"][l]).reshape(8, 128).T
        cst[l, :, C_FPRE:C_FPRE + 8] = np.asarray(inp["ffn_norm_pre"][l]).reshape(8, 128).T
        cst[l, :, C_FPOST:C_FPOST + 8] = np.asarray(inp["ffn_norm_post"][l]).reshape(8, 128).T
        cw = np.asarray(inp["ffn_conv_w"][l])
        for j in range(3):
            cst[l, :, C_CW + j * NFC:C_CW + (j + 1) * NFC] = cw[j].reshape(NFC, 128).T
        cst[l, :, C_CB:C_CB + NFC] = np.asarray(inp["ffn_conv_b"][l]).reshape(NFC, 128).T
    return cst


def make_in_maps(inp, k1x, k2, batches, nx_total=SEQ):
    ident, trimask, scanmask, table = host_consts()
    cst = make_cst(inp)
    meta = np.zeros((T0, D), np.float32)
    meta[T0 - NMETA:] = np.asarray(inp["meta_tokens"], np.float32)
    meta512 = np.zeros((TT, D), np.float32)
    meta512[TT - NMETA:] = np.asarray(inp["meta_tokens"], np.float32)
    f = lambda k: np.ascontiguousarray(np.asarray(inp[k], np.float32))
    shared = {
        "w_in": f("w_in"), "w_bh": f("w_branch_hgrn"), "w_bp": f("w_branch_pool"), "w_out": f("w_out"),
        "w_gate": f("ffn_w_gate"), "w_up": f("ffn_w_up"), "w_down": f("ffn_w_down"), "pproj": f("pool_proj"),
        "cst": cst, "ident": ident, "trimask": trimask, "scanmask": scanmask,
    }
    x = np.asarray(inp["x"], np.float32)
    mapsA, mapsB = [], []
    for b in batches:
        xb = x[b, :nx_total]
        a = dict(shared)
        a["meta"] = np.zeros((T0, D), np.float32)
        x1 = np.zeros((k1x * TT, D), np.float32)
        x1[(k1x - 1) * TT:] = meta512
        a["x1"] = x1
        x2 = np.zeros((k2 * TT, D), np.float32)
        na = min(k2 * TT, nx_total)
        x2[:na] = xb[:na]
        a["x2"] = x2
        a["invc1"] = table(T0, False)
        a["invc2"] = table(TT, True)
        mapsA.append(a)
        bm = dict(shared)
        bm["meta"] = meta
        bm["x1"] = np.ascontiguousarray(xb[:k1x * TT])
        x2 = np.zeros((k2 * TT, D), np.float32)
        nb = nx_total - k1x * TT
        x2[:nb] = xb[k1x * TT:]
        bm["x2"] = x2
        bm["invc1"] = table(T0, True)
        bm["invc2"] = table(TT, False)
        mapsB.append(bm)
    return mapsA + mapsB


K1X = 8
K2 = 8


def kernel(**inputs):
    nc = build_nc(K1X, K2, L)
    maps = make_in_maps(inputs, K1X, K2, [0, 1, 2, 3])
    order = []
    for b in range(4):
        order += [maps[b], maps[4 + b]]
    res = run_bass_kernel_spmd(nc, order, core_ids=list(range(8)))
    out = np.empty((4, SEQ, D), np.float32)
    for b in range(4):
        oa = np.asarray(res.results[2 * b]["out"], np.float32)
        ob = np.asarray(res.results[2 * b + 1]["out"], np.float32)
        out[b, :K2 * TT] = oa[:K2 * TT]
        out[b, K1X * TT:] = ob[:SEQ - K1X * TT]
    return out
```

```python
import contextlib
import numpy as np
import concourse.bass as bass
import concourse.mybir as mybir
from concourse.bass_utils import run_bass_kernel_spmd

F32 = mybir.dt.float32
BF16 = mybir.dt.bfloat16
AF = mybir.ActivationFunctionType
ALU = mybir.AluOpType

D = 1024
NDC = 8
TT = 512
DFF = 2816
NFC = 22
NH = 4
INC = 4608
L = 2
SEQ = 8192
NMETA = 16
T0 = 128
EPS = 1e-6
NSLOT = 5
PIECE = 4096
NPIECE = 32

C_R0, C_R1, C_OG, C_PS, C_MPRE, C_MPOST, C_FPRE, C_FPOST, C_CW, C_CB = 0, 4, 8, 12, 16, 24, 32, 40, 48, 114
C_LB, C_OML, C_NOML = 136, 140, 144
NCST = 148


class Tok:
    __slots__ = ("sem", "val")

    def __init__(self, sem, val):
        self.sem = sem
        self.val = val


class Trk:
    ENG = ("pe", "act", "dve", "pool", "sp")

    def __init__(self, nc, stack):
        self.nc = nc
        self.stack = stack
        self.thunks = {e: [] for e in self.ENG}
        self.cnt = {e: 0 for e in self.ENG}
        self.psem = {e: stack.enter_context(nc.semaphore("prog_" + e)) for e in self.ENG}
        self.waited = {e: {} for e in self.ENG}
        self.lastw = {}
        self.rds = {}
        self.dsem = {}
        self.nwaits = 0

    def _deps(self, reads, writes):
        deps = []
        for r in reads:
            t = self.lastw.get(r)
            if t is not None:
                deps.append(t)
        for w in writes:
            t = self.lastw.get(w)
            if t is not None:
                deps.append(t)
            deps.extend(self.rds.get(w, {}).values())
        return deps

    def _waits(self, eng, deps):
        need = {}
        for t in deps:
            if eng == "pe" and t.sem is self.psem["pe"]:
                continue
            k = id(t.sem)
            if k not in need or need[k][1] < t.val:
                need[k] = (t.sem, t.val)
        out = []
        for k, (sem, val) in need.items():
            if self.waited[eng].get(k, 0) >= val:
                continue
            self.waited[eng][k] = val
            out.append((sem, val))
        self.nwaits += len(out)
        return out

    def _record(self, tok, reads, writes):
        for r in reads:
            d = self.rds.setdefault(r, {})
            k = id(tok.sem)
            if k not in d or d[k].val < tok.val:
                d[k] = tok
        for w in writes:
            self.lastw[w] = tok
            self.rds[w] = {}

    def op(self, eng, fn, reads=(), writes=(), signal=True):
        waits = self._waits(eng, self._deps(reads, writes))
        if signal:
            self.cnt[eng] += 1
            tok = Tok(self.psem[eng], self.cnt[eng])
        else:
            tok = Tok(self.psem[eng], self.cnt[eng] + 1)
        sem = self.psem[eng]

        def thunk(e, fn=fn, waits=waits, signal=signal, sem=sem):
            for (s, v) in waits:
                e.wait_ge(s, v)
            ins = fn(e)
            if signal:
                ins.then_inc(sem, 1)

        self.thunks[eng].append(thunk)
        self._record(tok, reads, writes)
        return tok

    def dma(self, q, out, in_, reads=(), writes=(), key=None, record=True):
        waits = self._waits(q, self._deps(reads, writes))
        if key not in self.dsem:
            self.dsem[key] = [self.stack.enter_context(self.nc.semaphore("dma_%s" % (str(key).replace(" ", "")))), 0]
        s = self.dsem[key]
        s[1] += 16
        tok = Tok(s[0], s[1])
        sem = s[0]

        def thunk(e, out=out, in_=in_, waits=waits, sem=sem):
            for (sm, v) in waits:
                e.wait_ge(sm, v)
            e.dma_start(out=out, in_=in_).then_inc(sem, 16)

        self.thunks[q].append(thunk)
        if record:
            self._record(tok, reads, writes)
        return tok

    def wait_tok(self, eng, tok):
        waits = self._waits(eng, [tok])

        def thunk(e, waits=waits):
            for (s, v) in waits:
                e.wait_ge(s, v)

        self.thunks[eng].append(thunk)


def build_nc(k1x=8, k2=8, nlayers=L, stop=99):
    nc = bass.Bass("TRN2", target_bir_lowering=False)
    dr = {}
    dr["x1"] = nc.dram_tensor("x1", [max(k1x, 1) * TT, D], F32, kind="ExternalInput").ap()
    dr["meta"] = nc.dram_tensor("meta", [T0, D], F32, kind="ExternalInput").ap()
    dr["x2"] = nc.dram_tensor("x2", [k2 * TT, D], F32, kind="ExternalInput").ap()
    dr["w_in"] = nc.dram_tensor("w_in", [L, D, INC], F32, kind="ExternalInput").ap()
    dr["w_bh"] = nc.dram_tensor("w_bh", [L, 512, D], F32, kind="ExternalInput").ap()
    dr["w_bp"] = nc.dram_tensor("w_bp", [L, 512, D], F32, kind="ExternalInput").ap()
    dr["w_out"] = nc.dram_tensor("w_out", [L, D, D], F32, kind="ExternalInput").ap()
    dr["w_gate"] = nc.dram_tensor("w_gate", [L, D, DFF], F32, kind="ExternalInput").ap()
    dr["w_up"] = nc.dram_tensor("w_up", [L, D, DFF], F32, kind="ExternalInput").ap()
    dr["w_down"] = nc.dram_tensor("w_down", [L, DFF, D], F32, kind="ExternalInput").ap()
    dr["pproj"] = nc.dram_tensor("pproj", [L, 4, 128, 128], F32, kind="ExternalInput").ap()
    dr["cst"] = nc.dram_tensor("cst", [L, 128, NCST], F32, kind="ExternalInput").ap()
    dr["ident"] = nc.dram_tensor("ident", [128, 128], F32, kind="ExternalInput").ap()
    dr["trimask"] = nc.dram_tensor("trimask", [128, 128], F32, kind="ExternalInput").ap()
    dr["scanmask"] = nc.dram_tensor("scanmask", [128, TT], F32, kind="ExternalInput").ap()
    dr["invc1"] = nc.dram_tensor("invc1", [128, 4, T0], F32, kind="ExternalInput").ap()
    dr["invc2"] = nc.dram_tensor("invc2", [128, 4, TT], F32, kind="ExternalInput").ap()
    out = nc.dram_tensor("out", [k2 * TT, D], F32, kind="ExternalOutput").ap()
    wscr = nc.dram_tensor("wscr", [L, NPIECE, 128, PIECE], BF16, kind="Internal").ap()

    with contextlib.ExitStack() as st:
        T = Trk(nc, st)

        def sb(name, shape, dt):
            return st.enter_context(nc.sbuf_tensor(name, shape, dt))

        h = sb("h", [128, NDC, TT], F32)
        u = sb("u", [128, NDC, TT], BF16)
        sqz = sb("sqz", [128, NDC, TT], BF16)
        FQ = sb("FQ", [128, 4, TT], F32)
        FS = sb("FS", [128, 4, TT], F32)
        FK = sb("FK", [128, 4, TT], F32)
        FB = sb("FB", [128, 4, TT], F32)
        FE = sb("FE", [128, 4, TT], F32)
        BQ = sb("BQ", [128, 4, TT], BF16)
        BK = sb("BK", [128, 4, TT], BF16)
        BD = sb("BD", [128, 4, TT], BF16)
        BKD = sb("BKD", [128, 4, TT], BF16)
        BV = sb("BV", [128, 4, TT], BF16)
        BG = sb("BG", [128, 4, TT], BF16)
        BPL = BD
        BP = BK
        BA = BQ
        vp = [sb("vp%d" % l, [128, 4, 16 + TT], F32) for l in range(L)]
        Gb = [sb("Gb%d" % i, [128, 2 + TT], F32) for i in range(2)]
        CV = [sb("CV%d" % i, [128, 16 + TT], F32) for i in range(2)]
        P1, P2 = CV[0], CV[1]
        Gh = [sb("Gh%d" % l, [128, NFC, 2], F32) for l in range(L)]
        rs = sb("rs", [128, TT], F32)
        S = [sb("S%d" % l, [128, 2, NH, 128], F32) for l in range(L)]
        scur = [[0] * NH for _ in range(L)]
        Sdall = [sb("Sdall%d" % i, [128, 4, 128], BF16) for i in range(NH)]
        AT = [sb("AT%d" % i, [128, NH, 128], BF16) for i in range(2)]
        ex = sb("ex", [128, 2, NH, TT // 64], F32)
        identf = sb("identf", [128, 128], F32)
        identb = sb("identb", [128, 128], BF16)
        onesD = sb("onesD", [128, 128], BF16)
        onesV = sb("onesV", [128, 128], BF16)
        trim = sb("trim", [128, 128], F32)
        scanm = sb("scanm", [128, TT], F32)
        invc = sb("invc", [128, 4, TT], F32)
        epst = sb("epst", [128, 1], F32)
        tinyt = sb("tinyt", [128, 1], F32)
        invw = sb("invw", [128, 4], F32)
        cst = [sb("cst%d" % l, [128, NCST], F32) for l in range(L)]
        pproj = [sb("pproj%d" % l, [128, 4, 128], BF16) for l in range(L)]
        SGB = sb("SGB", [128, NDC, TT], BF16)
        SGA = sqz
        onesVf = sb("onesVf", [128, 128], F32)
        ring = [sb("ring%d" % i, [128, PIECE], BF16) for i in range(NSLOT)]

        psf = [st.enter_context(nc.psum_tensor("psf%d" % i, [128, 512], F32)) for i in range(7)]
        psb = st.enter_context(nc.psum_tensor("psb", [128, 1024], BF16))

        bank_ctr = [0]

        def bank(n=7, base=0):
            i = base + bank_ctr[0] % n
            bank_ctr[0] += 1
            return i

        T.dma("sp", identf[:], dr["ident"], writes=["identf"], key="c0")
        T.dma("sp", trim[:], dr["trimask"], writes=["trim"], key="c1")
        T.dma("sp", scanm[:], dr["scanmask"], writes=["scanm"], key="c2")
        T.dma("sp", invc[:, :, :T0], dr["invc1"], writes=["invc"], key="c3")
        for l in range(L):
            T.dma("sp", cst[l][:], dr["cst"][l], writes=[("cst", l)], key=("c4", l))
            T.dma("pool", pproj[l][:], dr["pproj"][l].rearrange("g c d -> c g d"), writes=[("pproj", l)], key=("c5", l))
        T.op("dve", lambda e: e.tensor_copy(out=identb[:], in_=identf[:]), reads=["identf"], writes=["identb"])
        T.op("pool", lambda e: e.memset(onesD[:], 1.0 / D), writes=["onesD"])
        T.op("pool", lambda e: e.memset(onesV[:], 1.0 / 128), writes=["onesV"])
        T.op("pool", lambda e: e.memset(onesVf[:], 1.0 / 128), writes=["onesVf"])
        T.op("pool", lambda e: e.memset(epst[:], EPS), writes=["epst"])
        T.op("pool", lambda e: e.memset(tinyt[:], 1e-30), writes=["tinyt"])
        for g in range(4):
            T.op("pool", lambda e, g=g: e.memset(invw[:, g:g + 1], 1.0 / 2 ** (g + 1)), writes=["invw"])
        T.op("pool", lambda e: e.memset(P1[:], 0.0), writes=[("CV", 0)])
        T.op("pool", lambda e: e.memset(P2[:], 0.0), writes=[("CV", 1)])
        for l in range(L):
            T.op("pool", lambda e, l=l: e.memset(S[l][:], 0.0), writes=[("S", l, hd, k) for hd in range(NH) for k in range(2)])
            T.op("pool", lambda e, l=l: e.memset(vp[l][:], 0.0), writes=[("vp", l)])
            T.op("pool", lambda e, l=l: e.memset(Gh[l][:], 0.0), writes=[("Gh", l)])
        T.op("dve", lambda e: e.memset(cst[0][:, C_LB:C_LB + 4], 0.0), reads=[("cst", 0)], writes=[("cst", 0)])
        if L > 1:
            T.op("dve", lambda e: e.tensor_tensor(out=cst[1][:, C_LB:C_LB + 4], in0=cst[1][:, C_R1:C_R1 + 4],
                                                  in1=cst[1][:, C_R0:C_R0 + 4], op=ALU.subtract),
                 reads=[("cst", 1)], writes=[("cst", 1)])
            T.op("act", lambda e: e.activation(out=cst[1][:, C_LB:C_LB + 4], in_=cst[1][:, C_LB:C_LB + 4], func=AF.Sigmoid),
                 reads=[("cst", 1)], writes=[("cst", 1)])
        for l in range(L):
            T.op("dve", lambda e, l=l: e.tensor_scalar(out=cst[l][:, C_OML:C_OML + 4], in0=cst[l][:, C_LB:C_LB + 4],
                                                       scalar1=-1.0, scalar2=1.0, op0=ALU.mult, op1=ALU.add),
                 reads=[("cst", l)], writes=[("cst", l)])
            T.op("dve", lambda e, l=l: e.tensor_scalar(out=cst[l][:, C_NOML:C_NOML + 4], in0=cst[l][:, C_LB:C_LB + 4],
                                                       scalar1=-1.0, scalar2=None, op0=ALU.add),
                 reads=[("cst", l)], writes=[("cst", l)])

        def kview(w2d):
            return w2d.rearrange("(kc p) n -> p kc n", p=128)

        def piece_segments(l, i):
            win = kview(dr["w_in"][l])
            if i < 5:
                return [(win[:, :, i * 512:(i + 1) * 512], 0, 8, 512)]
            if i == 5 or i == 8:
                j = 0 if i == 5 else 1
                return [(kview(dr["w_bh"][l])[:, :, j * 512:(j + 1) * 512], 0, 4, 512),
                        (kview(dr["w_bp"][l])[:, :, j * 512:(j + 1) * 512], 2048, 4, 512)]
            if i in (6, 7, 9, 10):
                j = {6: 0, 7: 1, 9: 2, 10: 3}[i]
                return [(win[:, :, 2560 + j * 256:2560 + (j + 1) * 256], 0, 8, 256),
                        (win[:, :, 3584 + j * 256:3584 + (j + 1) * 256], 2048, 8, 256)]
            if i in (11, 12):
                j = i - 11
                return [(kview(dr["w_out"][l])[:, :, j * 512:(j + 1) * 512], 0, 8, 512)]
            if i < 24:
                j = i - 13
                return [(kview(dr["w_gate"][l])[:, :, j * 256:(j + 1) * 256], 0, 8, 256),
                        (kview(dr["w_up"][l])[:, :, j * 256:(j + 1) * 256], 2048, 8, 256)]
            dc = i - 24
            return [(kview(dr["w_down"][l])[:, :, dc * 128:(dc + 1) * 128], 0, NFC, 128)]

        nq = [0]
        NPS = 76

        def cast_piece(l, i):
            for k, (src, off, kc, n) in enumerate(piece_segments(l, i)):
                dst = wscr[l, i, :, off:off + kc * n].rearrange("p (kc n) -> p kc n", n=n)
                T.dma("pool", dst, src, writes=[("prosem", nq[0] % NPS), ("scr" if k == 0 else "scr2", l, i)], key=("pro", nq[0] % NPS))
                nq[0] += 1

        deferred = []
        for l in range(nlayers):
            for i in range(NPIECE):
                if nlayers > 1 and l == nlayers - 1 and i not in (1, 2) and k1x >= 3:
                    deferred.append((l, i))
                else:
                    cast_piece(l, i)

        slot_ctr = [0]

        def load_piece(l, i):
            s = slot_ctr[0] % NSLOT
            slot_ctr[0] += 1
            n = PIECE if i < 24 else NFC * 128
            T.dma("sp", ring[s][:, :n], wscr[l, i, :, :n], reads=[("scr", l, i), ("scr2", l, i)], writes=[("ring", s)], key=("ring", s))
            return s

        def mm_group(ps_ap, pairs, reads, writes, each=None, skip=False, first=True, last=True):
            n = len(pairs)
            for k, (lh, rh) in enumerate(pairs):
                st_, sp_ = (k == 0 and first), (k == n - 1 and last)
                rr = list(reads) + (list(each[k]) if each is not None else [])
                if skip:
                    fn = lambda e, lh=lh, rh=rh, st_=st_, sp_=sp_: e.matmul(ps_ap, lhsT=lh, rhs=rh, start=st_, stop=sp_, skip_group_check=True)
                else:
                    fn = lambda e, lh=lh, rh=rh, st_=st_, sp_=sp_: e.matmul(ps_ap, lhsT=lh, rhs=rh, start=st_, stop=sp_)
                T.op("pe", fn, reads=rr, writes=writes, signal=(k == n - 1))

        def stats_from_squares(sqbuf, sqname, nchunks, T_, ones, onesname, out_ap, out_res):
            b = bank()
            mm_group(psf[b][:, :T_], [(ones[:], sqbuf[:, c, :T_]) for c in range(nchunks)], reads=[onesname], writes=[("psf", b)],
                     each=[[(sqname, c)] for c in range(nchunks)])
            T.op("act", lambda e: e.activation(out=out_ap, in_=psf[b][:, :T_], func=AF.Ln, bias=epst[:, 0:1]),
                 reads=[("psf", b), "epst"], writes=[out_res])
            T.op("act", lambda e: e.activation(out=out_ap, in_=out_ap, func=AF.Exp, scale=-0.5),
                 reads=[out_res], writes=[out_res])

        def prenorm(l, T_, gcol):
            for c in range(NDC):
                if c % 2 == 0:
                    T.op("act", lambda e, c=c: e.activation(out=sqz[:, c, :T_], in_=h[:, c, :T_], func=AF.Square),
                         reads=[("h", c)], writes=[("sqz", c)])
                else:
                    T.op("dve", lambda e, c=c: e.tensor_tensor(out=sqz[:, c, :T_], in0=h[:, c, :T_], in1=h[:, c, :T_], op=ALU.mult),
                         reads=[("h", c)], writes=[("sqz", c)])
            stats_from_squares(sqz, "sqz", NDC, T_, onesD, "onesD", rs[:, :T_], "rs")
            for c in range(NDC):
                T.op("dve", lambda e, c=c: e.scalar_tensor_tensor(out=u[:, c, :T_], in0=h[:, c, :T_],
                                                                 scalar=cst[l][:, gcol + c:gcol + c + 1], in1=rs[:, :T_],
                                                                 op0=ALU.mult, op1=ALU.mult),
                     reads=[("h", c), "rs", ("cst", l)], writes=[("u", c)])

        def y32(dc):
            return (FS if dc < 4 else FK)[:, dc % 4, :]

        def y32res(dc):
            return ("FS" if dc < 4 else "FK", dc % 4)

        def evac_y(dc, b, T_, sqbuf, sqname):
            T.op("dve", lambda e: e.tensor_copy(out=y32(dc)[:, :T_], in_=psf[b][:, :T_]), reads=[("psf", b)], writes=[y32res(dc)])
            T.op("pool", lambda e: e.tensor_tensor(out=sqbuf[:, dc, :T_], in0=y32(dc)[:, :T_], in1=y32(dc)[:, :T_], op=ALU.mult),
                 reads=[y32res(dc)], writes=[(sqname, dc)])

        def postnorm(l, T_, gcol, sqbuf, sqname):
            stats_from_squares(sqbuf, sqname, NDC, T_, onesD, "onesD", rs[:, :T_], "rs")
            for c in range(NDC):
                T.op("dve", lambda e, c=c: e.scalar_tensor_tensor(out=y32(c)[:, :T_], in0=y32(c)[:, :T_],
                                                                   scalar=cst[l][:, gcol + c:gcol + c + 1], in1=rs[:, :T_],
                                                                   op0=ALU.mult, op1=ALU.mult),
                     reads=["rs", ("cst", l)], writes=[y32res(c)])
                T.op("pool" if c % 2 == 0 else "dve", lambda e, c=c: e.tensor_tensor(out=h[:, c, :T_], in0=h[:, c, :T_], in1=y32(c)[:, :T_], op=ALU.add),
                     reads=[y32res(c)], writes=[("h", c)])

        def proj_fm(l, piece, T_, evac):
            s = load_piece(l, piece)
            W = ring[s]
            for hd in range(4):
                b = bank()
                mm_group(psf[b][:, :T_], [(W[:, kc * 512 + hd * 128:kc * 512 + (hd + 1) * 128], u[:, kc, :T_]) for kc in range(NDC)],
                         reads=[("ring", s)], writes=[("psf", b)], each=[[("u", kc)] for kc in range(NDC)])
                evac(hd, b)

        def mixer(l, T_, tile0, state_only=False, mid_hook=None):
            nblk = T_ // 128
            nch = T_ // 64
            c_ = cst[l]
            prenorm(l, T_, C_MPRE)
            if not state_only:
                proj_fm(l, 0, T_, lambda hd, b: T.op("act", lambda e: e.copy(out=FQ[:, hd, :T_], in_=psf[b][:, :T_]),
                                                     reads=[("psf", b)], writes=[("FQ", hd)]))
            proj_fm(l, 1, T_, lambda hd, b: T.op("act", lambda e: e.activation(out=FS[:, hd, :T_], in_=psf[b][:, :T_], func=AF.Sigmoid),
                                                 reads=[("psf", b)], writes=[("FS", hd)]))

            if mid_hook is not None:
                mid_hook()
            def hv(buf, hd):
                return buf[:, hd, :T_].rearrange("p (c j) -> p c j", j=64)

            def st_kk(hd):
                T.op("dve", lambda e: e.tensor_scalar(out=FK[:, hd, :T_], in0=FS[:, hd, :T_], scalar1=c_[:, C_NOML + hd:C_NOML + hd + 1],
                                                      scalar2=c_[:, C_OML + hd:C_OML + hd + 1], op0=ALU.mult, op1=ALU.add),
                     reads=[("FS", hd), ("cst", l)], writes=[("FK", hd)])

            def st_clamp(hd):
                T.op("dve", lambda e: e.tensor_scalar(out=FS[:, hd, :T_], in0=FS[:, hd, :T_], scalar1=tinyt[:, 0:1],
                                                      scalar2=c_[:, C_OML + hd:C_OML + hd + 1], op0=ALU.max, op1=ALU.mult),
                     reads=[("FS", hd), ("cst", l), "tinyt"], writes=[("FS", hd)])

            def st_ln(hd):
                T.op("act", lambda e: e.activation(out=FS[:, hd, :T_], in_=FS[:, hd, :T_], func=AF.Ln, bias=c_[:, C_LB + hd:C_LB + hd + 1]),
                     reads=[("FS", hd), ("cst", l)], writes=[("FS", hd)])

            def st_scan(hd):
                T.op("dve", lambda e: e.tensor_tensor_scan(out=FB[:, hd, :T_], data0=scanm[:, :T_], data1=FS[:, hd, :T_],
                                                           initial=0.0, op0=ALU.mult, op1=ALU.add),
                     reads=[("FS", hd), "scanm"], writes=[("FB", hd)])

            def st_ex(hd):
                FBh = hv(FB, hd)
                T.op("act", lambda e: e.activation(out=ex[:, 0, hd, :nch], in_=FBh[:, :, 31], func=AF.Exp), reads=[("FB", hd)], writes=[("ex0", hd)])
                T.op("act", lambda e: e.activation(out=ex[:, 1, hd, :nch], in_=FBh[:, :, 63], func=AF.Exp), reads=[("FB", hd)], writes=[("ex1", hd)])

            def st_d(hd):
                FBh, FSh = hv(FB, hd), hv(FS, hd)
                bmid = FBh[:, :, 31:32].broadcast_to([128, nch, 64])
                T.op("dve", lambda e: e.tensor_tensor(out=FSh, in0=FBh, in1=bmid, op=ALU.subtract), reads=[("FB", hd)], writes=[("FS", hd)])

            def st_e1(hd):
                T.op("act", lambda e: e.activation(out=FE[:, hd, :T_], in_=FS[:, hd, :T_], func=AF.Exp), reads=[("FS", hd)], writes=[("FE", hd)])

            def st_bq(hd):
                T.op("pool", lambda e: e.tensor_tensor(out=BQ[:, hd, :T_], in0=FQ[:, hd, :T_], in1=FE[:, hd, :T_], op=ALU.mult),
                     reads=[("FQ", hd), ("FE", hd)], writes=[("BQ", hd)])

            def st_e2(hd):
                T.op("act", lambda e: e.activation(out=FQ[:, hd, :T_], in_=FS[:, hd, :T_], func=AF.Exp, scale=-1.0), reads=[("FS", hd)], writes=[("FQ", hd)])

            def st_bk(hd):
                T.op("dve", lambda e: e.tensor_tensor(out=BK[:, hd, :T_], in0=FK[:, hd, :T_], in1=FQ[:, hd, :T_], op=ALU.mult),
                     reads=[("FK", hd), ("FQ", hd)], writes=[("BK", hd)])

            def st_d2(hd):
                FBh, FSh = hv(FB, hd), hv(FS, hd)
                blast = FBh[:, :, 63:64].broadcast_to([128, nch, 64])
                T.op("dve", lambda e: e.tensor_tensor(out=FSh, in0=blast, in1=FBh, op=ALU.subtract), reads=[("FB", hd)], writes=[("FS", hd)])

            def st_e3(hd):
                T.op("act", lambda e: e.activation(out=FE[:, hd, :T_], in_=FS[:, hd, :T_], func=AF.Exp), reads=[("FS", hd)], writes=[("FE", hd)])

            def st_bd(hd):
                T.op("pool", lambda e: e.tensor_tensor(out=BD[:, hd, :T_], in0=FK[:, hd, :T_], in1=FE[:, hd, :T_], op=ALU.mult),
                     reads=[("FK", hd), ("FE", hd)], writes=[("BD", hd)])

            def so_scan(hd):
                T.op("dve", lambda e: e.tensor_tensor_scan(out=FQ[:, hd, :T_], data0=scanm[:, :T_], data1=FS[:, hd, :T_],
                                                           initial=0.0, op0=ALU.mult, op1=ALU.add),
                     reads=[("FS", hd), "scanm"], writes=[("FQ", hd)])

            def so_ex(hd):
                Fh = hv(FQ, hd)
                T.op("act", lambda e: e.activation(out=ex[:, 1, hd, :nch], in_=Fh[:, :, 63], func=AF.Exp), reads=[("FQ", hd)], writes=[("ex1", hd)])

            def so_d2(hd):
                Fh, FSh = hv(FQ, hd), hv(FS, hd)
                blast = Fh[:, :, 63:64].broadcast_to([128, nch, 64])
                T.op("dve", lambda e: e.tensor_tensor(out=FSh, in0=blast, in1=Fh, op=ALU.subtract), reads=[("FQ", hd)], writes=[("FS", hd)])

            def so_e3(hd):
                T.op("act", lambda e: e.activation(out=FS[:, hd, :T_], in_=FS[:, hd, :T_], func=AF.Exp), reads=[("FS", hd)], writes=[("FS", hd)])

            def so_bd(hd):
                T.op("pool", lambda e: e.tensor_tensor(out=BD[:, hd, :T_], in0=FK[:, hd, :T_], in1=FS[:, hd, :T_], op=ALU.mult),
                     reads=[("FK", hd), ("FS", hd)], writes=[("BD", hd)])

            stages = [st_kk, st_clamp, st_ln, st_scan, st_ex, st_d, st_e1, st_bq, st_e2, st_bk, st_d2, st_e3, st_bd]

            def set_v():
                s = load_piece(l, 2)
                W = ring[s]
                for blk in range(nblk):
                    b = bank()
                    mm_group(psf[b][:, :512], [(u[:, kc, blk * 128:(blk + 1) * 128], W[:, kc * 512:(kc + 1) * 512]) for kc in range(NDC)],
                             reads=[("ring", s)], writes=[("psf", b)], each=[[("u", kc)] for kc in range(NDC)])
                    T.op("act", lambda e, blk=blk, b=b: e.copy(out=BV[:, blk, :], in_=psf[b][:, :512]),
                         reads=[("psf", b)], writes=[("BV", blk)])

            def set_g():
                proj_fm(l, 3, T_, lambda hd, b: T.op("act", lambda e: e.activation(out=BG[:, hd, :T_], in_=psf[b][:, :T_], func=AF.Sigmoid),
                                                     reads=[("psf", b)], writes=[("BG", hd)]))

            def set_vp():
                proj_fm(l, 4, T_, lambda hd, b: T.op("act", lambda e: e.copy(out=vp[l][:, hd, 16:16 + T_], in_=psf[b][:, :T_]),
                                                     reads=[("psf", b)], writes=[("vp", l)]))

            def set_gate(j4, pg):
                def f():
                    sg_ = load_piece(l, pg)
                    Wg = ring[sg_]
                    for sub in range(2):
                        dc = 2 * j4 + sub
                        co = sub * 128
                        bga, bgb = bank(), bank()
                        eu = [[("u", kc)] for kc in range(NDC)]
                        mm_group(psf[bga][:, :T_], [(Wg[:, kc * 256 + co:kc * 256 + co + 128], u[:, kc, :T_]) for kc in range(NDC)],
                                 reads=[("ring", sg_)], writes=[("psf", bga)], each=eu)
                        mm_group(psf[bgb][:, :T_], [(Wg[:, 2048 + kc * 256 + co:2048 + kc * 256 + co + 128], u[:, kc, :T_]) for kc in range(NDC)],
                                 reads=[("ring", sg_)], writes=[("psf", bgb)], each=eu)
                        T.op("act", lambda e, dc=dc, bga=bga: e.activation(out=SGA[:, dc, :T_], in_=psf[bga][:, :T_], func=AF.Sigmoid),
                             reads=[("psf", bga)], writes=[("sqz", dc)])
                        T.op("act", lambda e, dc=dc, bgb=bgb: e.activation(out=SGB[:, dc, :T_], in_=psf[bgb][:, :T_], func=AF.Sigmoid),
                             reads=[("psf", bgb)], writes=[("SGB", dc)])
                return f

            def set_pool():
                for g in range(4):
                    X = vp[l][:, g, :]
                    LL = 16 + T_
                    src = X
                    bufs = [P1, P2]
                    sh = 1
                    for k in range(g + 1):
                        dst = bufs[k % 2]
                        T.op("pool", lambda e, src=src, dst=dst, sh=sh: e.tensor_tensor(out=dst[:, sh:LL], in0=src[:, sh:LL], in1=src[:, 0:LL - sh], op=ALU.add),
                             reads=[("vp", l), ("CV", 0), ("CV", 1)], writes=[("CV", k % 2)])
                        src = dst
                        sh *= 2
                    R = src
                    if tile0:
                        in1 = invc[:, g, :T_]
                    else:
                        in1 = invw[:, g:g + 1].broadcast_to([128, T_])
                    T.op("pool", lambda e, R=R, in1=in1: e.tensor_tensor(out=R[:, 16:16 + T_], in0=R[:, 16:16 + T_], in1=in1, op=ALU.mult),
                         reads=[("CV", 0), ("CV", 1), "invc", "invw"], writes=[("CV", 0), ("CV", 1)])
                    T.op("pool", lambda e, R=R, g=g, X=X: e.tensor_tensor(out=BPL[:, g, :T_], in0=R[:, 16:16 + T_], in1=X[:, 16:16 + T_], op=ALU.subtract),
                         reads=[("CV", 0), ("CV", 1), ("vp", l)], writes=[("BD", g)])
                T.op("pool", lambda e: e.tensor_copy(out=vp[l][:, :, 0:16], in_=vp[l][:, :, T_:T_ + 16]), reads=[("vp", l)], writes=[("vp", l)])

            def set_tr():
                for blk in range(nblk):
                    half = blk % 2
                    for hd in range(NH):
                        T.op("pe", lambda e, blk=blk, hd=hd, half=half: e.transpose(out=psb[:, half * 512 + hd * 128:half * 512 + (hd + 1) * 128],
                                                                                    in_=BD[:, hd, blk * 128:(blk + 1) * 128], identity=identb[:]),
                             reads=[("BD", hd), "identb"], writes=["psb"], signal=(hd == NH - 1))
                    T.op("act", lambda e, blk=blk, half=half: e.copy(out=BKD[:, blk, :], in_=psb[:, half * 512:(half + 1) * 512]),
                         reads=["psb"], writes=[("BKD", blk)])

            pe_sets = [set_v, set_g, set_vp, set_gate(0, 6), set_gate(1, 7), set_gate(2, 9), set_gate(3, 10)]
            if state_only:
                stages = [st_kk, st_clamp, st_ln, so_scan, so_ex, so_d2, so_e3, so_bd]
                pe_sets = [set_v]
            for si, stg in enumerate(stages):
                for hd in range(NH):
                    stg(hd)
                if si % 2 == 0 and pe_sets:
                    pe_sets.pop(0)()
            while pe_sets:
                pe_sets.pop(0)()
            if state_only:
                def part2():
                    set_tr()
                    for pr in range(nblk):
                        bU0, bU1 = bank(), bank()
                        for j, bU in ((0, bU0), (1, bU1)):
                            for hd in range(NH):
                                T.op("pe", lambda e, hd=hd, j=j, bU=bU, pr=pr: e.matmul(psf[bU][:, hd * 128:(hd + 1) * 128],
                                                                                        lhsT=BKD[j * 64:(j + 1) * 64, pr, hd * 128:(hd + 1) * 128],
                                                                                        rhs=BV[j * 64:(j + 1) * 64, pr, hd * 128:(hd + 1) * 128], start=True, stop=True),
                                     reads=[("BKD", pr), ("BV", pr)], writes=[("psf", bU)], signal=(hd == NH - 1))
                        for j, bU in ((0, bU0), (1, bU1)):
                            c = 2 * pr + j
                            for hd in range(NH):
                                cu = scur[l][hd]
                                T.op("dve", lambda e, hd=hd, c=c, bU=bU, cu=cu: e.scalar_tensor_tensor(out=S[l][:, 1 - cu, hd, :], in0=S[l][:, cu, hd, :], scalar=ex[:, 1, hd, c:c + 1],
                                                                                                     in1=psf[bU][:, hd * 128:(hd + 1) * 128], op0=ALU.mult, op1=ALU.add),
                                     reads=[("psf", bU), ("ex1", hd), ("S", l, hd, cu)], writes=[("S", l, hd, 1 - cu)])
                                scur[l][hd] = 1 - cu
                return part2
            set_tr()
            set_pool()

            if stop < 5:
                return
            trim_bc = trim[:, None, :].broadcast_to([128, NH, 128]) if False else None
            pend = {}

            def SU(pr):
                bS, bU0, bU1 = bank(), bank(), bank()
                for hd in range(NH):
                    T.op("pe", lambda e, hd=hd: e.matmul(psf[bS][:, hd * 128:(hd + 1) * 128], lhsT=BK[:, hd, pr * 128:(pr + 1) * 128],
                                                         rhs=BQ[:, hd, pr * 128:(pr + 1) * 128], start=True, stop=True),
                         reads=[("BK", hd), ("BQ", hd)], writes=[("psf", bS)], signal=(hd == NH - 1))
                for j, bU in ((0, bU0), (1, bU1)):
                    for hd in range(NH):
                        T.op("pe", lambda e, hd=hd, j=j, bU=bU: e.matmul(psf[bU][:, hd * 128:(hd + 1) * 128],
                                                                         lhsT=BKD[j * 64:(j + 1) * 64, pr, hd * 128:(hd + 1) * 128],
                                                                         rhs=BV[j * 64:(j + 1) * 64, pr, hd * 128:(hd + 1) * 128], start=True, stop=True),
                             reads=[("BKD", pr), ("BV", pr)], writes=[("psf", bU)], signal=(hd == NH - 1))
                ai = pr % 2
                for hd in range(NH):
                    T.op("dve", lambda e, hd=hd: e.tensor_tensor(out=AT[ai][:, hd, :], in0=psf[bS][:, hd * 128:(hd + 1) * 128], in1=trim[:], op=ALU.mult),
                         reads=[("psf", bS), "trim"], writes=[("AT", ai, hd)])
                for j, bU in ((0, bU0), (1, bU1)):
                    c = 2 * pr + j
                    for hd in range(NH):
                        cu = scur[l][hd]
                        T.op("act", lambda e, hd=hd, c=c, cu=cu: e.mul(out=Sdall[hd][:, c % 4, :], in_=S[l][:, cu, hd, :], mul=ex[:, 0, hd, c:c + 1]),
                             reads=[("S", l, hd, cu), ("ex0", hd)], writes=[("Sd", hd, c % 4)])
                    for hd in range(NH):
                        cu = scur[l][hd]
                        T.op("dve", lambda e, hd=hd, c=c, bU=bU, cu=cu: e.scalar_tensor_tensor(out=S[l][:, 1 - cu, hd, :], in0=S[l][:, cu, hd, :], scalar=ex[:, 1, hd, c:c + 1],
                                                                                             in1=psf[bU][:, hd * 128:(hd + 1) * 128], op0=ALU.mult, op1=ALU.add),
                             reads=[("psf", bU), ("ex1", hd), ("S", l, hd, cu)], writes=[("S", l, hd, 1 - cu)])
                        scur[l][hd] = 1 - cu

            def O(pr):
                bO = bank()
                ai = pr % 2
                k = 0
                for hd in range(NH):
                    for j in range(2):
                        c = 2 * pr + j
                        T.op("pe", lambda e, hd=hd, c=c, j=j, k=k: e.matmul(psf[bO][:, hd * 128 + j * 64:hd * 128 + (j + 1) * 64], lhsT=Sdall[hd][:, c % 4, :],
                                                                           rhs=BQ[:, hd, c * 64:(c + 1) * 64], start=(k == 0), stop=False, skip_group_check=True),
                             reads=[("Sd", hd, c % 4), ("BQ", hd)], writes=[("psf", bO)], signal=False)
                        k += 1
                    T.op("pe", lambda e, hd=hd: e.matmul(psf[bO][:, hd * 128:(hd + 1) * 128], lhsT=BV[:, pr, hd * 128:(hd + 1) * 128], rhs=AT[ai][:, hd, :],
                                                         start=False, stop=(hd == NH - 1), skip_group_check=True),
                         reads=[("BV", pr), ("AT", ai, hd)], writes=[("psf", bO)], signal=(hd == NH - 1))
                pv = psf[bO][:, :512].rearrange("p (h t) -> p h t", t=128)
                T.op("act", lambda e: e.copy(out=FQ[:, :, pr * 128:(pr + 1) * 128], in_=pv), reads=[("psf", bO)], writes=[("FQ", hd) for hd in range(NH)])
                T.op("act", lambda e: e.activation(out=FE[:, :, pr * 128:(pr + 1) * 128], in_=pv, func=AF.Square),
                     reads=[("psf", bO)], writes=[("FE", hd) for hd in range(NH)])

            SU(0)
            for pr in range(nblk):
                if pr + 1 < nblk:
                    SU(pr + 1)
                O(pr)
            if stop < 6:
                return
            allF = lambda n: [(n, hd) for hd in range(NH)]
            for hd in range(NH):
                b = bank()
                mm_group(psf[b][:, :T_], [(onesVf[:], FE[:, hd, :T_])], reads=[("FE", hd), "onesVf"], writes=[("psf", b)])
                T.op("act", lambda e, hd=hd, b=b: e.activation(out=FS[:, hd, :T_], in_=psf[b][:, :T_], func=AF.Ln, bias=epst[:, 0:1]),
                     reads=[("psf", b), "epst"], writes=[("FS", hd)])
                T.op("act", lambda e, hd=hd: e.activation(out=FS[:, hd, :T_], in_=FS[:, hd, :T_], func=AF.Exp, scale=-0.5),
                     reads=[("FS", hd)], writes=[("FS", hd)])
                T.op("pool", lambda e, hd=hd: e.tensor_tensor(out=FQ[:, hd, :T_], in0=FQ[:, hd, :T_], in1=FS[:, hd, :T_], op=ALU.mult),
                     reads=[("FS", hd), ("FQ", hd)], writes=[("FQ", hd)])
                T.op("dve", lambda e, hd=hd: e.scalar_tensor_tensor(out=BA[:, hd, :T_], in0=FQ[:, hd, :T_], scalar=c_[:, C_OG + hd:C_OG + hd + 1],
                                                                    in1=BG[:, hd, :T_], op0=ALU.mult, op1=ALU.mult),
                     reads=[("FQ", hd), ("BG", hd), ("cst", l)], writes=[("BQ", hd)])
            if stop < 7:
                return
            for g in range(4):
                b = bank()
                mm_group(psf[b][:, :T_], [(pproj[l][:, g, :], BPL[:, g, :T_])], reads=[("pproj", l), ("BD", g)], writes=[("psf", b)])
                T.op("act", lambda e, g=g, b=b: e.mul(out=BP[:, g, :T_], in_=psf[b][:, :T_], mul=c_[:, C_PS + g:C_PS + g + 1]),
                     reads=[("psf", b), ("cst", l)], writes=[("BK", g)])

            if stop < 8:
                return
            for half in range(2):
                sb_ = load_piece(l, 5 if half == 0 else 8)
                Wb = ring[sb_]
                for q4 in range(4):
                    dc = half * 4 + q4
                    cb = q4 * 128
                    bbh, bbp = bank(), bank()
                    mm_group(psf[bbh][:, :T_], [(Wb[:, kc * 512 + cb:kc * 512 + cb + 128], BA[:, kc, :T_]) for kc in range(4)],
                             reads=[("ring", sb_)], writes=[("psf", bbh)], each=[[("BQ", kc)] for kc in range(4)])
                    mm_group(psf[bbp][:, :T_], [(Wb[:, 2048 + kc * 512 + cb:2048 + kc * 512 + cb + 128], BP[:, kc, :T_]) for kc in range(4)],
                             reads=[("ring", sb_)], writes=[("psf", bbp)], each=[[("BK", kc)] for kc in range(4)])
                    za = FQ[:, dc % 2, :T_]
                    zb = FQ[:, 2 + dc % 2, :T_]
                    T.op("dve", lambda e, za=za, bbh=bbh, dc=dc: e.tensor_tensor(out=za, in0=SGA[:, dc, :T_], in1=psf[bbh][:, :T_], op=ALU.mult),
                         reads=[("psf", bbh), ("sqz", dc)], writes=[("FQ", dc % 2)])
                    T.op("dve", lambda e, zb=zb, bbp=bbp, dc=dc: e.tensor_tensor(out=zb, in0=SGB[:, dc, :T_], in1=psf[bbp][:, :T_], op=ALU.mult),
                         reads=[("psf", bbp), ("SGB", dc)], writes=[("FQ", 2 + dc % 2)])
                    T.op("pool", lambda e, za=za, zb=zb, dc=dc: e.tensor_tensor(out=sqz[:, dc, :T_], in0=za, in1=zb, op=ALU.add),
                         reads=[("FQ", dc % 2), ("FQ", 2 + dc % 2)], writes=[("sqz", dc)])
            if stop < 9:
                return
            for half in range(2):
                s = load_piece(l, 11 + half)
                W = ring[s]
                for q4 in range(4):
                    dc = half * 4 + q4
                    b = bank()
                    mm_group(psf[b][:, :T_], [(W[:, kc * 512 + q4 * 128:kc * 512 + (q4 + 1) * 128], sqz[:, kc, :T_]) for kc in range(NDC)],
                             reads=[("ring", s)], writes=[("psf", b)], each=[[("sqz", kc)] for kc in range(NDC)])
                    evac_y(dc, b, T_, u, "u")
            postnorm(l, T_, C_MPOST, u, "u")

        def ffn(l, T_):
            if stop < 10:
                return
            c_ = cst[l]
            prenorm(l, T_, C_FPRE)
            eu = [[("u", kc)] for kc in range(NDC)]

            def mtile(fc):
                blkt = [BQ, BK, BD, BKD, BV, BG][fc // 4]
                return blkt[:, fc % 4, :T_], (["BQ", "BK", "BD", "BKD", "BV", "BG"][fc // 4], fc % 4)

            for j in range(11):
                s = load_piece(l, 13 + j)
                W = ring[s]
                for sub in range(2):
                    fc = 2 * j + sub
                    bg, bu = bank(), bank()
                    mm_group(psf[bg][:, :T_], [(W[:, kc * 256 + sub * 128:kc * 256 + (sub + 1) * 128], u[:, kc, :T_]) for kc in range(NDC)],
                             reads=[("ring", s)], writes=[("psf", bg)], each=eu)
                    mm_group(psf[bu][:, :T_], [(W[:, 2048 + kc * 256 + sub * 128:2048 + kc * 256 + (sub + 1) * 128], u[:, kc, :T_]) for kc in range(NDC)],
                             reads=[("ring", s)], writes=[("psf", bu)], each=eu)
                    gi = fc % 2
                    G = Gb[gi]
                    cv = CV[gi]
                    w0 = c_[:, C_CW + fc:C_CW + fc + 1]
                    w1 = c_[:, C_CW + NFC + fc:C_CW + NFC + fc + 1]
                    w2 = c_[:, C_CW + 2 * NFC + fc:C_CW + 2 * NFC + fc + 1]
                    bb_ = c_[:, C_CB + fc:C_CB + fc + 1]
                    T.op("pool", lambda e, G=G, fc=fc: e.tensor_copy(out=G[:, 0:2], in_=Gh[l][:, fc, :]), reads=[("Gh", l)], writes=[("Gbh", gi)])
                    T.op("act", lambda e, G=G, bg=bg: e.copy(out=G[:, 2:2 + T_], in_=psf[bg][:, :T_]), reads=[("psf", bg)], writes=[("Gb", gi)])
                    T.op("act", lambda e, cv=cv, bg=bg, w2=w2, bb_=bb_: e.activation(out=cv[:, :T_], in_=psf[bg][:, :T_], func=AF.Identity, scale=w2, bias=bb_),
                         reads=[("psf", bg), ("cst", l)], writes=[("CV", gi)])
                    T.op("pool", lambda e, G=G, fc=fc: e.tensor_copy(out=Gh[l][:, fc, :], in_=G[:, T_:T_ + 2]), reads=[("Gb", gi)], writes=[("Gh", l)])
                    T.op("dve", lambda e, G=G, cv=cv, w1=w1: e.scalar_tensor_tensor(out=cv[:, :T_], in0=G[:, 1:1 + T_], scalar=w1, in1=cv[:, :T_],
                                                                                  op0=ALU.mult, op1=ALU.add),
                         reads=[("Gb", gi), ("Gbh", gi), ("CV", gi), ("cst", l)], writes=[("CV", gi)])
                    T.op("dve", lambda e, G=G, cv=cv, w0=w0: e.scalar_tensor_tensor(out=cv[:, :T_], in0=G[:, 0:T_], scalar=w0, in1=cv[:, :T_],
                                                                                  op0=ALU.mult, op1=ALU.add),
                         reads=[("Gb", gi), ("Gbh", gi), ("CV", gi), ("cst", l)], writes=[("CV", gi)])
                    T.op("act", lambda e, cv=cv: e.activation(out=cv[:, :T_], in_=cv[:, :T_], func=AF.Gelu_apprx_tanh),
                         reads=[("CV", gi)], writes=[("CV", gi)])
                    mt, mres = mtile(fc)
                    T.op("dve", lambda e, cv=cv, mt=mt, bu=bu: e.tensor_tensor(out=mt, in0=cv[:, :T_], in1=psf[bu][:, :T_], op=ALU.mult),
                         reads=[("CV", gi), ("psf", bu)], writes=[mres])
            for dc in range(NDC):
                s = load_piece(l, 24 + dc)
                W = ring[s]
                b = bank()
                pairs = []
                each = []
                for fc in range(NFC):
                    mt, mres = mtile(fc)
                    pairs.append((W[:, fc * 128:(fc + 1) * 128], mt))
                    each.append([mres])
                mm_group(psf[b][:, :T_], pairs, reads=[("ring", s)], writes=[("psf", b)], each=each)
                evac_y(dc, b, T_, sqz, "sqz")
            postnorm(l, T_, C_FPOST, sqz, "sqz")

        def tokhalf(hf, store=False):
            if store:
                return (FS if hf == 0 else FK), ("FS" if hf == 0 else "FK")
            return (FB if hf == 0 else FE), ("FB" if hf == 0 else "FE")

        def load_dma(src_rows, T_):
            nblk = T_ // 128
            srcv = src_rows.rearrange("(b p) (hf f) -> p b hf f", p=128, hf=2)
            for hf in range(2):
                tb, tn = tokhalf(hf)
                T.dma("sp", tb[:, :nblk, :], srcv[:, :, hf, :], writes=[(tn, k) for k in range(4)], key=("in", hf))

        def load_tile(T_):
            nblk = T_ // 128
            for dc in range(NDC):
                tb, tn = tokhalf(dc // 4)
                b = bank()
                for blk in range(nblk):
                    T.op("pe", lambda e, tb=tb, dc=dc, blk=blk, b=b: e.transpose(out=psf[b][:, blk * 128:(blk + 1) * 128],
                                                                                 in_=tb[:, blk, (dc % 4) * 128:(dc % 4 + 1) * 128], identity=identf[:]),
                         reads=[(tn, blk), "identf"], writes=[("psf", b)], signal=(blk == nblk - 1))
                if dc % 2 == 0:
                    T.op("act", lambda e, dc=dc, b=b: e.copy(out=h[:, dc, :T_], in_=psf[b][:, :T_]), reads=[("psf", b)], writes=[("h", dc)])
                else:
                    T.op("dve", lambda e, dc=dc, b=b: e.tensor_copy(out=h[:, dc, :T_], in_=psf[b][:, :T_]), reads=[("psf", b)], writes=[("h", dc)])

        def store_tile(dst_rows, T_):
            nblk = T_ // 128
            dstv = dst_rows.rearrange("(b p) (hf f) -> p b hf f", p=128, hf=2)
            toks = []
            for hf in range(2):
                tb, tn = tokhalf(hf, store=True)
                for blk in range(nblk):
                    b = bank()
                    for q4 in range(4):
                        dc = hf * 4 + q4
                        T.op("pe", lambda e, dc=dc, blk=blk, b=b, q4=q4: e.transpose(out=psf[b][:, q4 * 128:(q4 + 1) * 128],
                                                                                   in_=h[:, dc, blk * 128:(blk + 1) * 128], identity=identf[:]),
                             reads=[("h", dc), "identf"], writes=[("psf", b)], signal=(q4 == 3))
                    if blk % 2 == 0:
                        T.op("act", lambda e, tb=tb, blk=blk, b=b: e.copy(out=tb[:, blk, :], in_=psf[b][:, :512]), reads=[("psf", b)], writes=[(tn, blk)])
                    else:
                        T.op("dve", lambda e, tb=tb, blk=blk, b=b: e.tensor_copy(out=tb[:, blk, :], in_=psf[b][:, :512]), reads=[("psf", b)], writes=[(tn, blk)])
                toks.append(T.dma("sp", dstv[:, :, hf, :], tb[:, :nblk, :], reads=[(tn, k) for k in range(4)], key=("out", hf)))
            return toks

        last = []
        seq1 = [("meta", None, T0)] + [("x1", t, TT) for t in range(k1x)]
        seq2 = [("x2", t, TT) for t in range(k2)]

        def src_of(item):
            nm, t, T_ = item
            return dr[nm] if t is None else dr[nm][t * TT:(t + 1) * TT, :]

        items = [(1, it) for it in seq1] + [(2, it) for it in seq2]
        load_dma(src_of(items[0][1]), items[0][1][2])
        first2 = True
        pending = [None]
        for idx, (ph, it) in enumerate(items):
            T_ = it[2]
            last_prefix = (ph == 1 and idx == len(seq1) - 1)
            is_meta = (ph == 1 and idx == 0) or last_prefix
            if last_prefix:
                T.dma("sp", invc[:], dr["invc2"], writes=["invc"], key="c3")
            load_tile(T_)
            if ph == 1 and idx >= 1 and deferred:
                nper = -(-30 // max(1, (k1x - 2)))
                for _ in range(min(nper, len(deferred))):
                    cast_piece(*deferred.pop(0))
            for l in range(nlayers):
                so = (ph == 1 and l == nlayers - 1 and not last_prefix and nlayers > 1)
                if so:
                    pending[0] = mixer(l, T_, is_meta, state_only=True)
                else:
                    hook, pending[0] = (pending[0], None) if l == 0 else (None, pending[0])
                    mixer(l, T_, is_meta, mid_hook=hook)
                if l == nlayers - 1 and idx + 1 < len(items):
                    load_dma(src_of(items[idx + 1][1]), items[idx + 1][1][2])
                if not so:
                    ffn(l, T_)
            if ph == 2:
                first2 = False
                last = store_tile(out[it[1] * TT:(it[1] + 1) * TT, :], T_)
        if pending[0] is not None:
            pending[0]()
        assert not deferred
        for tk in last:
            T.wait_tok("sp", tk)

        with nc.Block() as block:
            @block.sync
            def _(e):
                for th in T.thunks["sp"]:
                    th(e)

            @block.tensor
            def _(e):
                for th in T.thunks["pe"]:
                    th(e)

            @block.scalar
            def _(e):
                for th in T.thunks["act"]:
                    th(e)

            @block.vector
            def _(e):
                for th in T.thunks["dve"]:
                    th(e)

            @block.gpsimd
            def _(e):
                for th in T.thunks["pool"]:
                    th(e)
    return nc


def host_consts():
    ident = np.eye(128, dtype=np.float32)
    s = np.arange(128)[:, None]
    t = np.arange(128)[None, :]
    trimask = ((s // 64 == t // 64) & (s <= t)).astype(np.float32)
    scanmask = np.ones((128, TT), np.float32)
    scanmask[:, ::64] = 0.0

    def table(width, with_meta):
        tb = np.ones((128, 4, width), np.float32)
        for g, w in enumerate((2, 4, 8, 16)):
            if with_meta:
                pos = np.arange(width) - (width - NMETA) + 1
                cnt = np.minimum(np.maximum(pos, 1), w).astype(np.float32)
            else:
                cnt = np.full(width, w, np.float32)
            tb[:, g, :] = (1.0 / cnt)[None, :]
        return tb
    return ident, trimask, scanmask, table


def make_cst(inp):
    cst = np.zeros((L, 128, NCST), np.float32)
    lbr = np.asarray(inp["hgrn_lower_bounds"], np.float32)
    for l in range(L):
        cst[l, :, C_R0:C_R0 + 4] = lbr[0].reshape(4, 128).T
        cst[l, :, C_R1:C_R1 + 4] = lbr[1].reshape(4, 128).T
        cst[l, :, C_OG:C_OG + 4] = np.asarray(inp["hgrn_out_norm"][l]).reshape(4, 128).T
        cst[l, :, C_PS:C_PS + 4] = np.asarray(inp["pool_scale"][l]).reshape(4, 128).T
        cst[l, :, C_MPRE:C_MPRE + 8] = np.asarray(inp["mix_norm_pre"][l]).reshape(8, 128).T
        cst[l, :, C_MPOST:C_MPOST + 8] = np.asarray(inp["mix_norm_post"][l]).reshape(8, 128).T
        cst[l, :, C_FPRE:C_FPRE + 8] = np.asarray(inp["ffn_norm_pre"][l]).reshape(8, 128).T
        cst[l, :, C_FPOST:C_FPOST + 8] = np.asarray(inp["ffn_norm_post"][l]).reshape(8, 128).T
        cw = np.asarray(inp["ffn_conv_w"][l])
        for j in range(3):
            cst[l, :, C_CW + j * NFC:C_CW + (j + 1) * NFC] = cw[j].reshape(NFC, 128).T
        cst[l, :, C_CB:C_CB + NFC] = np.asarray(inp["ffn_conv_b"][l]).reshape(NFC, 128).T
    return cst


def make_in_maps(inp, k1x, k2, batches, nx_total=SEQ):
    ident, trimask, scanmask, table = host_consts()
    cst = make_cst(inp)
    meta = np.zeros((T0, D), np.float32)
    meta[T0 - NMETA:] = np.asarray(inp["meta_tokens"], np.float32)
    meta512 = np.zeros((TT, D), np.float32)
    meta512[TT - NMETA:] = np.asarray(inp["meta_tokens"], np.float32)
    f = lambda k: np.ascontiguousarray(np.asarray(inp[k], np.float32))
    shared = {
        "w_in": f("w_in"), "w_bh": f("w_branch_hgrn"), "w_bp": f("w_branch_pool"), "w_out": f("w_out"),
        "w_gate": f("ffn_w_gate"), "w_up": f("ffn_w_up"), "w_down": f("ffn_w_down"), "pproj": f("pool_proj"),
        "cst": cst, "ident": ident, "trimask": trimask, "scanmask": scanmask,
    }
    x = np.asarray(inp["x"], np.float32)
    mapsA, mapsB = [], []
    for b in batches:
        xb = x[b, :nx_total]
        a = dict(shared)
        a["meta"] = np.zeros((T0, D), np.float32)
        x1 = np.zeros((k1x * TT, D), np.float32)
        x1[(k1x - 1) * TT:] = meta512
        a["x1"] = x1
        x2 = np.zeros((k2 * TT, D), np.float32)
        na = min(k2 * TT, nx_total)
        x2[:na] = xb[:na]
        a["x2"] = x2
        a["invc1"] = table(T0, False)
        a["invc2"] = table(TT, True)
        mapsA.append(a)
        bm = dict(shared)
        bm["meta"] = meta
        bm["x1"] = np.ascontiguousarray(xb[:k1x * TT])
        x2 = np.zeros((k2 * TT, D), np.float32)
        nb = nx_total - k1x * TT
        x2[:nb] = xb[k1x * TT:]
        bm["x2"] = x2
        bm["invc1"] = table(T0, True)
        bm["invc2"] = table(TT, False)
        mapsB.append(bm)
    return mapsA + mapsB


K1X = 8
K2 = 8


def kernel(**inputs):
    nc = build_nc(K1X, K2, L)
    maps = make_in_maps(inputs, K1X, K2, [0, 1, 2, 3])
    order = []
    for b in range(4):
        order += [maps[b], maps[4 + b]]
    res = run_bass_kernel_spmd(nc, order, core_ids=list(range(8)))
    out = np.empty((4, SEQ, D), np.float32)
    for b in range(4):
        oa = np.asarray(res.results[2 * b]["out"], np.float32)
        ob = np.asarray(res.results[2 * b + 1]["out"], np.float32)
        out[b, :K2 * TT] = oa[:K2 * TT]
        out[b, K1X * TT:] = ob[:SEQ - K1X * TT]
    return out
```

```python
import contextlib
import numpy as np
import concourse.bass as bass
import concourse.mybir as mybir
from concourse.bass_utils import run_bass_kernel_spmd

F32 = mybir.dt.float32
BF16 = mybir.dt.bfloat16
AF = mybir.ActivationFunctionType
ALU = mybir.AluOpType

D = 1024
NDC = 8
TT = 512
DFF = 2816
NFC = 22
NH = 4
INC = 4608
L = 2
SEQ = 8192
NMETA = 16
T0 = 128
EPS = 1e-6
NSLOT = 5
PIECE = 4096
NPIECE = 32

C_R0, C_R1, C_OG, C_PS, C_MPRE, C_MPOST, C_FPRE, C_FPOST, C_CW, C_CB = 0, 4, 8, 12, 16, 24, 32, 40, 48, 114
C_LB, C_OML, C_NOML = 136, 140, 144
NCST = 148


class Tok:
    __slots__ = ("sem", "val")

    def __init__(self, sem, val):
        self.sem = sem
        self.val = val


class Trk:
    ENG = ("pe", "act", "dve", "pool", "sp")

    def __init__(self, nc, stack):
        self.nc = nc
        self.stack = stack
        self.thunks = {e: [] for e in self.ENG}
        self.cnt = {e: 0 for e in self.ENG}
        self.psem = {e: stack.enter_context(nc.semaphore("prog_" + e)) for e in self.ENG}
        self.waited = {e: {} for e in self.ENG}
        self.lastw = {}
        self.rds = {}
        self.dsem = {}
        self.nwaits = 0

    def _deps(self, reads, writes):
        deps = []
        for r in reads:
            t = self.lastw.get(r)
            if t is not None:
                deps.append(t)
        for w in writes:
            t = self.lastw.get(w)
            if t is not None:
                deps.append(t)
            deps.extend(self.rds.get(w, {}).values())
        return deps

    def _waits(self, eng, deps):
        need = {}
        for t in deps:
            if eng == "pe" and t.sem is self.psem["pe"]:
                continue
            k = id(t.sem)
            if k not in need or need[k][1] < t.val:
                need[k] = (t.sem, t.val)
        out = []
        for k, (sem, val) in need.items():
            if self.waited[eng].get(k, 0) >= val:
                continue
            self.waited[eng][k] = val
            out.append((sem, val))
        self.nwaits += len(out)
        return out

    def _record(self, tok, reads, writes):
        for r in reads:
            d = self.rds.setdefault(r, {})
            k = id(tok.sem)
            if k not in d or d[k].val < tok.val:
                d[k] = tok
        for w in writes:
            self.lastw[w] = tok
            self.rds[w] = {}

    def op(self, eng, fn, reads=(), writes=(), signal=True):
        waits = self._waits(eng, self._deps(reads, writes))
        if signal:
            self.cnt[eng] += 1
            tok = Tok(self.psem[eng], self.cnt[eng])
        else:
            tok = Tok(self.psem[eng], self.cnt[eng] + 1)
        sem = self.psem[eng]

        def thunk(e, fn=fn, waits=waits, signal=signal, sem=sem):
            for (s, v) in waits:
                e.wait_ge(s, v)
            ins = fn(e)
            if signal:
                ins.then_inc(sem, 1)

        self.thunks[eng].append(thunk)
        self._record(tok, reads, writes)
        return tok

    def dma(self, q, out, in_, reads=(), writes=(), key=None, record=True):
        waits = self._waits(q, self._deps(reads, writes))
        if key not in self.dsem:
            self.dsem[key] = [self.stack.enter_context(self.nc.semaphore("dma_%s" % (str(key).replace(" ", "")))), 0]
        s = self.dsem[key]
        s[1] += 16
        tok = Tok(s[0], s[1])
        sem = s[0]

        def thunk(e, out=out, in_=in_, waits=waits, sem=sem):
            for (sm, v) in waits:
                e.wait_ge(sm, v)
            e.dma_start(out=out, in_=in_).then_inc(sem, 16)

        self.thunks[q].append(thunk)
        if record:
            self._record(tok, reads, writes)
        return tok

    def wait_tok(self, eng, tok):
        waits = self._waits(eng, [tok])

        def thunk(e, waits=waits):
            for (s, v) in waits:
                e.wait_ge(s, v)

        self.thunks[eng].append(thunk)


def build_nc(k1x=8, k2=8, nlayers=L, stop=99):
    nc = bass.Bass("TRN2", target_bir_lowering=False)
    dr = {}
    dr["x1"] = nc.dram_tensor("x1", [max(k1x, 1) * TT, D], F32, kind="ExternalInput").ap()
    dr["meta"] = nc.dram_tensor("meta", [T0, D], F32, kind="ExternalInput").ap()
    dr["x2"] = nc.dram_tensor("x2", [k2 * TT, D], F32, kind="ExternalInput").ap()
    dr["w_in"] = nc.dram_tensor("w_in", [L, D, INC], F32, kind="ExternalInput").ap()
    dr["w_bh"] = nc.dram_tensor("w_bh", [L, 512, D], F32, kind="ExternalInput").ap()
    dr["w_bp"] = nc.dram_tensor("w_bp", [L, 512, D], F32, kind="ExternalInput").ap()
    dr["w_out"] = nc.dram_tensor("w_out", [L, D, D], F32, kind="ExternalInput").ap()
    dr["w_gate"] = nc.dram_tensor("w_gate", [L, D, DFF], F32, kind="ExternalInput").ap()
    dr["w_up"] = nc.dram_tensor("w_up", [L, D, DFF], F32, kind="ExternalInput").ap()
    dr["w_down"] = nc.dram_tensor("w_down", [L, DFF, D], F32, kind="ExternalInput").ap()
    dr["pproj"] = nc.dram_tensor("pproj", [L, 4, 128, 128], F32, kind="ExternalInput").ap()
    dr["cst"] = nc.dram_tensor("cst", [L, 128, NCST], F32, kind="ExternalInput").ap()
    dr["ident"] = nc.dram_tensor("ident", [128, 128], F32, kind="ExternalInput").ap()
    dr["trimask"] = nc.dram_tensor("trimask", [128, 128], F32, kind="ExternalInput").ap()
    dr["scanmask"] = nc.dram_tensor("scanmask", [128, TT], F32, kind="ExternalInput").ap()
    dr["invc1"] = nc.dram_tensor("invc1", [128, 4, T0], F32, kind="ExternalInput").ap()
    dr["invc2"] = nc.dram_tensor("invc2", [128, 4, TT], F32, kind="ExternalInput").ap()
    out = nc.dram_tensor("out", [k2 * TT, D], F32, kind="ExternalOutput").ap()
    wscr = nc.dram_tensor("wscr", [L, NPIECE, 128, PIECE], BF16, kind="Internal").ap()

    with contextlib.ExitStack() as st:
        T = Trk(nc, st)

        def sb(name, shape, dt):
            return st.enter_context(nc.sbuf_tensor(name, shape, dt))

        h = sb("h", [128, NDC, TT], F32)
        u = sb("u", [128, NDC, TT], BF16)
        sqz = sb("sqz", [128, NDC, TT], BF16)
        FQ = sb("FQ", [128, 4, TT], F32)
        FS = sb("FS", [128, 4, TT], F32)
        FK = sb("FK", [128, 4, TT], F32)
        FB = sb("FB", [128, 4, TT], F32)
        FE = sb("FE", [128, 4, TT], F32)
        BQ = sb("BQ", [128, 4, TT], BF16)
        BK = sb("BK", [128, 4, TT], BF16)
        BD = sb("BD", [128, 4, TT], BF16)
        BKD = sb("BKD", [128, 4, TT], BF16)
        BV = sb("BV", [128, 4, TT], BF16)
        BG = sb("BG", [128, 4, TT], BF16)
        BPL = BD
        BP = BK
        BA = BQ
        vp = [sb("vp%d" % l, [128, 4, 16 + TT], F32) for l in range(L)]
        Gb = [sb("Gb%d" % i, [128, 2 + TT], F32) for i in range(2)]
        CV = [sb("CV%d" % i, [128, 16 + TT], F32) for i in range(2)]
        P1, P2 = CV[0], CV[1]
        Gh = [sb("Gh%d" % l, [128, NFC, 2], F32) for l in range(L)]
        rs = sb("rs", [128, TT], F32)
        S = [sb("S%d" % l, [128, 2, NH, 128], F32) for l in range(L)]
        scur = [[0] * NH for _ in range(L)]
        Sdall = [sb("Sdall%d" % i, [128, 4, 128], BF16) for i in range(NH)]
        AT = [sb("AT%d" % i, [128, NH, 128], BF16) for i in range(2)]
        ex = sb("ex", [128, 2, NH, TT // 64], F32)
        identf = sb("identf", [128, 128], F32)
        identb = sb("identb", [128, 128], BF16)
        onesD = sb("onesD", [128, 128], BF16)
        onesV = sb("onesV", [128, 128], BF16)
        trim = sb("trim", [128, 128], F32)
        scanm = sb("scanm", [128, TT], F32)
        invc = sb("invc", [128, 4, TT], F32)
        epst = sb("epst", [128, 1], F32)
        tinyt = sb("tinyt", [128, 1], F32)
        invw = sb("invw", [128, 4], F32)
        cst = [sb("cst%d" % l, [128, NCST], F32) for l in range(L)]
        pproj = [sb("pproj%d" % l, [128, 4, 128], BF16) for l in range(L)]
        SGB = sb("SGB", [128, NDC, TT], BF16)
        SGA = sqz
        onesVf = sb("onesVf", [128, 128], F32)
        ring = [sb("ring%d" % i, [128, PIECE], BF16) for i in range(NSLOT)]

        psf = [st.enter_context(nc.psum_tensor("psf%d" % i, [128, 512], F32)) for i in range(7)]
        psb = st.enter_context(nc.psum_tensor("psb", [128, 1024], BF16))

        bank_ctr = [0]

        def bank(n=7, base=0):
            i = base + bank_ctr[0] % n
            bank_ctr[0] += 1
            return i

        T.dma("sp", identf[:], dr["ident"], writes=["identf"], key="c0")
        T.dma("sp", trim[:], dr["trimask"], writes=["trim"], key="c1")
        T.dma("sp", scanm[:], dr["scanmask"], writes=["scanm"], key="c2")
        T.dma("sp", invc[:, :, :T0], dr["invc1"], writes=["invc"], key="c3")
        for l in range(L):
            T.dma("sp", cst[l][:], dr["cst"][l], writes=[("cst", l)], key=("c4", l))
            T.dma("pool", pproj[l][:], dr["pproj"][l].rearrange("g c d -> c g d"), writes=[("pproj", l)], key=("c5", l))
        T.op("dve", lambda e: e.tensor_copy(out=identb[:], in_=identf[:]), reads=["identf"], writes=["identb"])
        T.op("pool", lambda e: e.memset(onesD[:], 1.0 / D), writes=["onesD"])
        T.op("pool", lambda e: e.memset(onesV[:], 1.0 / 128), writes=["onesV"])
        T.op("pool", lambda e: e.memset(onesVf[:], 1.0 / 128), writes=["onesVf"])
        T.op("pool", lambda e: e.memset(epst[:], EPS), writes=["epst"])
        T.op("pool", lambda e: e.memset(tinyt[:], 1e-30), writes=["tinyt"])
        for g in range(4):
            T.op("pool", lambda e, g=g: e.memset(invw[:, g:g + 1], 1.0 / 2 ** (g + 1)), writes=["invw"])
        T.op("pool", lambda e: e.memset(P1[:], 0.0), writes=[("CV", 0)])
        T.op("pool", lambda e: e.memset(P2[:], 0.0), writes=[("CV", 1)])
        for l in range(L):
            T.op("pool", lambda e, l=l: e.memset(S[l][:], 0.0), writes=[("S", l, hd, k) for hd in range(NH) for k in range(2)])
            T.op("pool", lambda e, l=l: e.memset(vp[l][:], 0.0), writes=[("vp", l)])
            T.op("pool", lambda e, l=l: e.memset(Gh[l][:], 0.0), writes=[("Gh", l)])
        T.op("dve", lambda e: e.memset(cst[0][:, C_LB:C_LB + 4], 0.0), reads=[("cst", 0)], writes=[("cst", 0)])
        if L > 1:
            T.op("dve", lambda e: e.tensor_tensor(out=cst[1][:, C_LB:C_LB + 4], in0=cst[1][:, C_R1:C_R1 + 4],
                                                  in1=cst[1][:, C_R0:C_R0 + 4], op=ALU.subtract),
                 reads=[("cst", 1)], writes=[("cst", 1)])
            T.op("act", lambda e: e.activation(out=cst[1][:, C_LB:C_LB + 4], in_=cst[1][:, C_LB:C_LB + 4], func=AF.Sigmoid),
                 reads=[("cst", 1)], writes=[("cst", 1)])
        for l in range(L):
            T.op("dve", lambda e, l=l: e.tensor_scalar(out=cst[l][:, C_OML:C_OML + 4], in0=cst[l][:, C_LB:C_LB + 4],
                                                       scalar1=-1.0, scalar2=1.0, op0=ALU.mult, op1=ALU.add),
                 reads=[("cst", l)], writes=[("cst", l)])
            T.op("dve", lambda e, l=l: e.tensor_scalar(out=cst[l][:, C_NOML:C_NOML + 4], in0=cst[l][:, C_LB:C_LB + 4],
                                                       scalar1=-1.0, scalar2=None, op0=ALU.add),
                 reads=[("cst", l)], writes=[("cst", l)])

        def kview(w2d):
            return w2d.rearrange("(kc p) n -> p kc n", p=128)

        def piece_segments(l, i):
            win = kview(dr["w_in"][l])
            if i < 5:
                return [(win[:, :, i * 512:(i + 1) * 512], 0, 8, 512)]
            if i == 5 or i == 8:
                j = 0 if i == 5 else 1
                return [(kview(dr["w_bh"][l])[:, :, j * 512:(j + 1) * 512], 0, 4, 512),
                        (kview(dr["w_bp"][l])[:, :, j * 512:(j + 1) * 512], 2048, 4, 512)]
            if i in (6, 7, 9, 10):
                j = {6: 0, 7: 1, 9: 2, 10: 3}[i]
                return [(win[:, :, 2560 + j * 256:2560 + (j + 1) * 256], 0, 8, 256),
                        (win[:, :, 3584 + j * 256:3584 + (j + 1) * 256], 2048, 8, 256)]
            if i in (11, 12):
                j = i - 11
                return [(kview(dr["w_out"][l])[:, :, j * 512:(j + 1) * 512], 0, 8, 512)]
            if i < 24:
                j = i - 13
                return [(kview(dr["w_gate"][l])[:, :, j * 256:(j + 1) * 256], 0, 8, 256),
                        (kview(dr["w_up"][l])[:, :, j * 256:(j + 1) * 256], 2048, 8, 256)]
            dc = i - 24
            return [(kview(dr["w_down"][l])[:, :, dc * 128:(dc + 1) * 128], 0, NFC, 128)]

        nq = [0]
        NPS = 76

        def cast_piece(l, i):
            for k, (src, off, kc, n) in enumerate(piece_segments(l, i)):
                dst = wscr[l, i, :, off:off + kc * n].rearrange("p (kc n) -> p kc n", n=n)
                T.dma("pool", dst, src, writes=[("prosem", nq[0] % NPS), ("scr" if k == 0 else "scr2", l, i)], key=("pro", nq[0] % NPS))
                nq[0] += 1

        deferred = []
        for l in range(nlayers):
            for i in range(NPIECE):
                if nlayers > 1 and l == nlayers - 1 and i not in (1, 2) and k1x >= 3:
                    deferred.append((l, i))
                else:
                    cast_piece(l, i)

        slot_ctr = [0]

        def load_piece(l, i):
            s = slot_ctr[0] % NSLOT
            slot_ctr[0] += 1
            n = PIECE if i < 24 else NFC * 128
            T.dma("sp", ring[s][:, :n], wscr[l, i, :, :n], reads=[("scr", l, i), ("scr2", l, i)], writes=[("ring", s)], key=("ring", s))
            return s

        def mm_group(ps_ap, pairs, reads, writes, each=None, skip=False, first=True, last=True):
            n = len(pairs)
            for k, (lh, rh) in enumerate(pairs):
                st_, sp_ = (k == 0 and first), (k == n - 1 and last)
                rr = list(reads) + (list(each[k]) if each is not None else [])
                if skip:
                    fn = lambda e, lh=lh, rh=rh, st_=st_, sp_=sp_: e.matmul(ps_ap, lhsT=lh, rhs=rh, start=st_, stop=sp_, skip_group_check=True)
                else:
                    fn = lambda e, lh=lh, rh=rh, st_=st_, sp_=sp_: e.matmul(ps_ap, lhsT=lh, rhs=rh, start=st_, stop=sp_)
                T.op("pe", fn, reads=rr, writes=writes, signal=(k == n - 1))

        def stats_from_squares(sqbuf, sqname, nchunks, T_, ones, onesname, out_ap, out_res):
            b = bank()
            mm_group(psf[b][:, :T_], [(ones[:], sqbuf[:, c, :T_]) for c in range(nchunks)], reads=[onesname], writes=[("psf", b)],
                     each=[[(sqname, c)] for c in range(nchunks)])
            T.op("act", lambda e: e.activation(out=out_ap, in_=psf[b][:, :T_], func=AF.Ln, bias=epst[:, 0:1]),
                 reads=[("psf", b), "epst"], writes=[out_res])
            T.op("act", lambda e: e.activation(out=out_ap, in_=out_ap, func=AF.Exp, scale=-0.5),
                 reads=[out_res], writes=[out_res])

        def prenorm(l, T_, gcol):
            for c in range(NDC):
                T.op("act", lambda e, c=c: e.activation(out=sqz[:, c, :T_], in_=h[:, c, :T_], func=AF.Square),
                     reads=[("h", c)], writes=[("sqz", c)])
            stats_from_squares(sqz, "sqz", NDC, T_, onesD, "onesD", rs[:, :T_], "rs")
            for c in range(NDC):
                T.op("dve", lambda e, c=c: e.scalar_tensor_tensor(out=u[:, c, :T_], in0=h[:, c, :T_],
                                                                 scalar=cst[l][:, gcol + c:gcol + c + 1], in1=rs[:, :T_],
                                                                 op0=ALU.mult, op1=ALU.mult),
                     reads=[("h", c), "rs", ("cst", l)], writes=[("u", c)])

        def y32(dc):
            return (FS if dc < 4 else FK)[:, dc % 4, :]

        def y32res(dc):
            return ("FS" if dc < 4 else "FK", dc % 4)

        def evac_y(dc, b, T_, sqbuf, sqname):
            T.op("dve", lambda e: e.tensor_copy(out=y32(dc)[:, :T_], in_=psf[b][:, :T_]), reads=[("psf", b)], writes=[y32res(dc)])
            T.op("pool", lambda e: e.tensor_tensor(out=sqbuf[:, dc, :T_], in0=y32(dc)[:, :T_], in1=y32(dc)[:, :T_], op=ALU.mult),
                 reads=[y32res(dc)], writes=[(sqname, dc)])

        def postnorm(l, T_, gcol, sqbuf, sqname):
            stats_from_squares(sqbuf, sqname, NDC, T_, onesD, "onesD", rs[:, :T_], "rs")
            for c in range(NDC):
                T.op("dve", lambda e, c=c: e.scalar_tensor_tensor(out=y32(c)[:, :T_], in0=y32(c)[:, :T_],
                                                                   scalar=cst[l][:, gcol + c:gcol + c + 1], in1=rs[:, :T_],
                                                                   op0=ALU.mult, op1=ALU.mult),
                     reads=["rs", ("cst", l)], writes=[y32res(c)])
                T.op("pool" if c % 2 == 0 else "dve", lambda e, c=c: e.tensor_tensor(out=h[:, c, :T_], in0=h[:, c, :T_], in1=y32(c)[:, :T_], op=ALU.add),
                     reads=[y32res(c)], writes=[("h", c)])

        def proj_fm(l, piece, T_, evac):
            s = load_piece(l, piece)
            W = ring[s]
            for hd in range(4):
                b = bank()
                mm_group(psf[b][:, :T_], [(W[:, kc * 512 + hd * 128:kc * 512 + (hd + 1) * 128], u[:, kc, :T_]) for kc in range(NDC)],
                         reads=[("ring", s)], writes=[("psf", b)], each=[[("u", kc)] for kc in range(NDC)])
                evac(hd, b)

        def mixer(l, T_, tile0, state_only=False, mid_hook=None):
            nblk = T_ // 128
            nch = T_ // 64
            c_ = cst[l]
            prenorm(l, T_, C_MPRE)
            if not state_only:
                proj_fm(l, 0, T_, lambda hd, b: T.op("act", lambda e: e.copy(out=FQ[:, hd, :T_], in_=psf[b][:, :T_]),
                                                     reads=[("psf", b)], writes=[("FQ", hd)]))
            proj_fm(l, 1, T_, lambda hd, b: T.op("act", lambda e: e.activation(out=FS[:, hd, :T_], in_=psf[b][:, :T_], func=AF.Sigmoid),
                                                 reads=[("psf", b)], writes=[("FS", hd)]))

            if mid_hook is not None:
                mid_hook()
            def hv(buf, hd):
                return buf[:, hd, :T_].rearrange("p (c j) -> p c j", j=64)

            def st_kk(hd):
                T.op("dve", lambda e: e.tensor_scalar(out=FK[:, hd, :T_], in0=FS[:, hd, :T_], scalar1=c_[:, C_NOML + hd:C_NOML + hd + 1],
                                                      scalar2=c_[:, C_OML + hd:C_OML + hd + 1], op0=ALU.mult, op1=ALU.add),
                     reads=[("FS", hd), ("cst", l)], writes=[("FK", hd)])

            def st_clamp(hd):
                T.op("dve", lambda e: e.tensor_scalar(out=FS[:, hd, :T_], in0=FS[:, hd, :T_], scalar1=tinyt[:, 0:1],
                                                      scalar2=c_[:, C_OML + hd:C_OML + hd + 1], op0=ALU.max, op1=ALU.mult),
                     reads=[("FS", hd), ("cst", l), "tinyt"], writes=[("FS", hd)])

            def st_ln(hd):
                T.op("act", lambda e: e.activation(out=FS[:, hd, :T_], in_=FS[:, hd, :T_], func=AF.Ln, bias=c_[:, C_LB + hd:C_LB + hd + 1]),
                     reads=[("FS", hd), ("cst", l)], writes=[("FS", hd)])

            def st_scan(hd):
                T.op("dve", lambda e: e.tensor_tensor_scan(out=FB[:, hd, :T_], data0=scanm[:, :T_], data1=FS[:, hd, :T_],
                                                           initial=0.0, op0=ALU.mult, op1=ALU.add),
                     reads=[("FS", hd), "scanm"], writes=[("FB", hd)])

            def st_ex(hd):
                FBh = hv(FB, hd)
                T.op("act", lambda e: e.activation(out=ex[:, 0, hd, :nch], in_=FBh[:, :, 31], func=AF.Exp), reads=[("FB", hd)], writes=[("ex0", hd)])
                T.op("act", lambda e: e.activation(out=ex[:, 1, hd, :nch], in_=FBh[:, :, 63], func=AF.Exp), reads=[("FB", hd)], writes=[("ex1", hd)])

            def st_d(hd):
                FBh, FSh = hv(FB, hd), hv(FS, hd)
                bmid = FBh[:, :, 31:32].broadcast_to([128, nch, 64])
                T.op("dve", lambda e: e.tensor_tensor(out=FSh, in0=FBh, in1=bmid, op=ALU.subtract), reads=[("FB", hd)], writes=[("FS", hd)])

            def st_e1(hd):
                T.op("act", lambda e: e.activation(out=FE[:, hd, :T_], in_=FS[:, hd, :T_], func=AF.Exp), reads=[("FS", hd)], writes=[("FE", hd)])

            def st_bq(hd):
                T.op("pool", lambda e: e.tensor_tensor(out=BQ[:, hd, :T_], in0=FQ[:, hd, :T_], in1=FE[:, hd, :T_], op=ALU.mult),
                     reads=[("FQ", hd), ("FE", hd)], writes=[("BQ", hd)])

            def st_e2(hd):
                T.op("act", lambda e: e.activation(out=FQ[:, hd, :T_], in_=FS[:, hd, :T_], func=AF.Exp, scale=-1.0), reads=[("FS", hd)], writes=[("FQ", hd)])

            def st_bk(hd):
                T.op("dve", lambda e: e.tensor_tensor(out=BK[:, hd, :T_], in0=FK[:, hd, :T_], in1=FQ[:, hd, :T_], op=ALU.mult),
                     reads=[("FK", hd), ("FQ", hd)], writes=[("BK", hd)])

            def st_d2(hd):
                FBh, FSh = hv(FB, hd), hv(FS, hd)
                blast = FBh[:, :, 63:64].broadcast_to([128, nch, 64])
                T.op("dve", lambda e: e.tensor_tensor(out=FSh, in0=blast, in1=FBh, op=ALU.subtract), reads=[("FB", hd)], writes=[("FS", hd)])

            def st_e3(hd):
                T.op("act", lambda e: e.activation(out=FE[:, hd, :T_], in_=FS[:, hd, :T_], func=AF.Exp), reads=[("FS", hd)], writes=[("FE", hd)])

            def st_bd(hd):
                T.op("pool", lambda e: e.tensor_tensor(out=BD[:, hd, :T_], in0=FK[:, hd, :T_], in1=FE[:, hd, :T_], op=ALU.mult),
                     reads=[("FK", hd), ("FE", hd)], writes=[("BD", hd)])

            def so_scan(hd):
                T.op("dve", lambda e: e.tensor_tensor_scan(out=FQ[:, hd, :T_], data0=scanm[:, :T_], data1=FS[:, hd, :T_],
                                                           initial=0.0, op0=ALU.mult, op1=ALU.add),
                     reads=[("FS", hd), "scanm"], writes=[("FQ", hd)])

            def so_ex(hd):
                Fh = hv(FQ, hd)
                T.op("act", lambda e: e.activation(out=ex[:, 1, hd, :nch], in_=Fh[:, :, 63], func=AF.Exp), reads=[("FQ", hd)], writes=[("ex1", hd)])

            def so_d2(hd):
                Fh, FSh = hv(FQ, hd), hv(FS, hd)
                blast = Fh[:, :, 63:64].broadcast_to([128, nch, 64])
                T.op("dve", lambda e: e.tensor_tensor(out=FSh, in0=blast, in1=Fh, op=ALU.subtract), reads=[("FQ", hd)], writes=[("FS", hd)])

            def so_e3(hd):
                T.op("act", lambda e: e.activation(out=FS[:, hd, :T_], in_=FS[:, hd, :T_], func=AF.Exp), reads=[("FS", hd)], writes=[("FS", hd)])

            def so_bd(hd):
                T.op("pool", lambda e: e.tensor_tensor(out=BD[:, hd, :T_], in0=FK[:, hd, :T_], in1=FS[:, hd, :T_], op=ALU.mult),
                     reads=[("FK", hd), ("FS", hd)], writes=[("BD", hd)])

            stages = [st_kk, st_clamp, st_ln, st_scan, st_ex, st_d, st_e1, st_bq, st_e2, st_bk, st_d2, st_e3, st_bd]

            def set_v():
                s = load_piece(l, 2)
                W = ring[s]
                for blk in range(nblk):
                    b = bank()
                    mm_group(psf[b][:, :512], [(u[:, kc, blk * 128:(blk + 1) * 128], W[:, kc * 512:(kc + 1) * 512]) for kc in range(NDC)],
                             reads=[("ring", s)], writes=[("psf", b)], each=[[("u", kc)] for kc in range(NDC)])
                    T.op("act", lambda e, blk=blk, b=b: e.copy(out=BV[:, blk, :], in_=psf[b][:, :512]),
                         reads=[("psf", b)], writes=[("BV", blk)])

            def set_g():
                proj_fm(l, 3, T_, lambda hd, b: T.op("act", lambda e: e.activation(out=BG[:, hd, :T_], in_=psf[b][:, :T_], func=AF.Sigmoid),
                                                     reads=[("psf", b)], writes=[("BG", hd)]))

            def set_vp():
                proj_fm(l, 4, T_, lambda hd, b: T.op("act", lambda e: e.copy(out=vp[l][:, hd, 16:16 + T_], in_=psf[b][:, :T_]),
                                                     reads=[("psf", b)], writes=[("vp", l)]))

            def set_gate(j4, pg):
                def f():
                    sg_ = load_piece(l, pg)
                    Wg = ring[sg_]
                    for sub in range(2):
                        dc = 2 * j4 + sub
                        co = sub * 128
                        bga, bgb = bank(), bank()
                        eu = [[("u", kc)] for kc in range(NDC)]
                        mm_group(psf[bga][:, :T_], [(Wg[:, kc * 256 + co:kc * 256 + co + 128], u[:, kc, :T_]) for kc in range(NDC)],
                                 reads=[("ring", sg_)], writes=[("psf", bga)], each=eu)
                        mm_group(psf[bgb][:, :T_], [(Wg[:, 2048 + kc * 256 + co:2048 + kc * 256 + co + 128], u[:, kc, :T_]) for kc in range(NDC)],
                                 reads=[("ring", sg_)], writes=[("psf", bgb)], each=eu)
                        T.op("act", lambda e, dc=dc, bga=bga: e.activation(out=SGA[:, dc, :T_], in_=psf[bga][:, :T_], func=AF.Sigmoid),
                             reads=[("psf", bga)], writes=[("sqz", dc)])
                        T.op("act", lambda e, dc=dc, bgb=bgb: e.activation(out=SGB[:, dc, :T_], in_=psf[bgb][:, :T_], func=AF.Sigmoid),
                             reads=[("psf", bgb)], writes=[("SGB", dc)])
                return f

            def set_pool():
                for g in range(4):
                    X = vp[l][:, g, :]
                    LL = 16 + T_
                    src = X
                    bufs = [P1, P2]
                    sh = 1
                    for k in range(g + 1):
                        dst = bufs[k % 2]
                        T.op("pool", lambda e, src=src, dst=dst, sh=sh: e.tensor_tensor(out=dst[:, sh:LL], in0=src[:, sh:LL], in1=src[:, 0:LL - sh], op=ALU.add),
                             reads=[("vp", l), ("CV", 0), ("CV", 1)], writes=[("CV", k % 2)])
                        src = dst
                        sh *= 2
                    R = src
                    if tile0:
                        in1 = invc[:, g, :T_]
                    else:
                        in1 = invw[:, g:g + 1].broadcast_to([128, T_])
                    T.op("pool", lambda e, R=R, in1=in1: e.tensor_tensor(out=R[:, 16:16 + T_], in0=R[:, 16:16 + T_], in1=in1, op=ALU.mult),
                         reads=[("CV", 0), ("CV", 1), "invc", "invw"], writes=[("CV", 0), ("CV", 1)])
                    T.op("pool", lambda e, R=R, g=g, X=X: e.tensor_tensor(out=BPL[:, g, :T_], in0=R[:, 16:16 + T_], in1=X[:, 16:16 + T_], op=ALU.subtract),
                         reads=[("CV", 0), ("CV", 1), ("vp", l)], writes=[("BD", g)])
                T.op("pool", lambda e: e.tensor_copy(out=vp[l][:, :, 0:16], in_=vp[l][:, :, T_:T_ + 16]), reads=[("vp", l)], writes=[("vp", l)])

            def set_tr():
                for blk in range(nblk):
                    half = blk % 2
                    for hd in range(NH):
                        T.op("pe", lambda e, blk=blk, hd=hd, half=half: e.transpose(out=psb[:, half * 512 + hd * 128:half * 512 + (hd + 1) * 128],
                                                                                    in_=BD[:, hd, blk * 128:(blk + 1) * 128], identity=identb[:]),
                             reads=[("BD", hd), "identb"], writes=["psb"], signal=(hd == NH - 1))
                    T.op("act", lambda e, blk=blk, half=half: e.copy(out=BKD[:, blk, :], in_=psb[:, half * 512:(half + 1) * 512]),
                         reads=["psb"], writes=[("BKD", blk)])

            pe_sets = [set_v, set_g, set_vp, set_gate(0, 6), set_gate(1, 7), set_gate(2, 9), set_gate(3, 10)]
            if state_only:
                stages = [st_kk, st_clamp, st_ln, so_scan, so_ex, so_d2, so_e3, so_bd]
                pe_sets = [set_v]
            for si, stg in enumerate(stages):
                for hd in range(NH):
                    stg(hd)
                if si % 2 == 0 and pe_sets:
                    pe_sets.pop(0)()
            while pe_sets:
                pe_sets.pop(0)()
            if state_only:
                def part2():
                    set_tr()
                    for pr in range(nblk):
                        bU0, bU1 = bank(), bank()
                        for j, bU in ((0, bU0), (1, bU1)):
                            for hd in range(NH):
                                T.op("pe", lambda e, hd=hd, j=j, bU=bU, pr=pr: e.matmul(psf[bU][:, hd * 128:(hd + 1) * 128],
                                                                                        lhsT=BKD[j * 64:(j + 1) * 64, pr, hd * 128:(hd + 1) * 128],
                                                                                        rhs=BV[j * 64:(j + 1) * 64, pr, hd * 128:(hd + 1) * 128], start=True, stop=True),
                                     reads=[("BKD", pr), ("BV", pr)], writes=[("psf", bU)], signal=(hd == NH - 1))
                        for j, bU in ((0, bU0), (1, bU1)):
                            c = 2 * pr + j
                            for hd in range(NH):
                                cu = scur[l][hd]
                                T.op("dve", lambda e, hd=hd, c=c, bU=bU, cu=cu: e.scalar_tensor_tensor(out=S[l][:, 1 - cu, hd, :], in0=S[l][:, cu, hd, :], scalar=ex[:, 1, hd, c:c + 1],
                                                                                                     in1=psf[bU][:, hd * 128:(hd + 1) * 128], op0=ALU.mult, op1=ALU.add),
                                     reads=[("psf", bU), ("ex1", hd), ("S", l, hd, cu)], writes=[("S", l, hd, 1 - cu)])
                                scur[l][hd] = 1 - cu
                return part2
            set_tr()
            set_pool()

            if stop < 5:
                return
            trim_bc = trim[:, None, :].broadcast_to([128, NH, 128]) if False else None
            pend = {}

            def SU(pr):
                bS, bU0, bU1 = bank(), bank(), bank()
                for hd in range(NH):
                    T.op("pe", lambda e, hd=hd: e.matmul(psf[bS][:, hd * 128:(hd + 1) * 128], lhsT=BK[:, hd, pr * 128:(pr + 1) * 128],
                                                         rhs=BQ[:, hd, pr * 128:(pr + 1) * 128], start=True, stop=True),
                         reads=[("BK", hd), ("BQ", hd)], writes=[("psf", bS)], signal=(hd == NH - 1))
                for j, bU in ((0, bU0), (1, bU1)):
                    for hd in range(NH):
                        T.op("pe", lambda e, hd=hd, j=j, bU=bU: e.matmul(psf[bU][:, hd * 128:(hd + 1) * 128],
                                                                         lhsT=BKD[j * 64:(j + 1) * 64, pr, hd * 128:(hd + 1) * 128],
                                                                         rhs=BV[j * 64:(j + 1) * 64, pr, hd * 128:(hd + 1) * 128], start=True, stop=True),
                             reads=[("BKD", pr), ("BV", pr)], writes=[("psf", bU)], signal=(hd == NH - 1))
                ai = pr % 2
                for hd in range(NH):
                    T.op("dve", lambda e, hd=hd: e.tensor_tensor(out=AT[ai][:, hd, :], in0=psf[bS][:, hd * 128:(hd + 1) * 128], in1=trim[:], op=ALU.mult),
                         reads=[("psf", bS), "trim"], writes=[("AT", ai, hd)])
                for j, bU in ((0, bU0), (1, bU1)):
                    c = 2 * pr + j
                    for hd in range(NH):
                        cu = scur[l][hd]
                        T.op("act", lambda e, hd=hd, c=c, cu=cu: e.mul(out=Sdall[hd][:, c % 4, :], in_=S[l][:, cu, hd, :], mul=ex[:, 0, hd, c:c + 1]),
                             reads=[("S", l, hd, cu), ("ex0", hd)], writes=[("Sd", hd, c % 4)])
                    for hd in range(NH):
                        cu = scur[l][hd]
                        T.op("dve", lambda e, hd=hd, c=c, bU=bU, cu=cu: e.scalar_tensor_tensor(out=S[l][:, 1 - cu, hd, :], in0=S[l][:, cu, hd, :], scalar=ex[:, 1, hd, c:c + 1],
                                                                                             in1=psf[bU][:, hd * 128:(hd + 1) * 128], op0=ALU.mult, op1=ALU.add),
                             reads=[("psf", bU), ("ex1", hd), ("S", l, hd, cu)], writes=[("S", l, hd, 1 - cu)])
                        scur[l][hd] = 1 - cu

            def O(pr):
                bO = bank()
                ai = pr % 2
                k = 0
                for hd in range(NH):
                    for j in range(2):
                        c = 2 * pr + j
                        T.op("pe", lambda e, hd=hd, c=c, j=j, k=k: e.matmul(psf[bO][:, hd * 128 + j * 64:hd * 128 + (j + 1) * 64], lhsT=Sdall[hd][:, c % 4, :],
                                                                           rhs=BQ[:, hd, c * 64:(c + 1) * 64], start=(k == 0), stop=False, skip_group_check=True),
                             reads=[("Sd", hd, c % 4), ("BQ", hd)], writes=[("psf", bO)], signal=False)
                        k += 1
                    T.op("pe", lambda e, hd=hd: e.matmul(psf[bO][:, hd * 128:(hd + 1) * 128], lhsT=BV[:, pr, hd * 128:(hd + 1) * 128], rhs=AT[ai][:, hd, :],
                                                         start=False, stop=(hd == NH - 1), skip_group_check=True),
                         reads=[("BV", pr), ("AT", ai, hd)], writes=[("psf", bO)], signal=(hd == NH - 1))
                pv = psf[bO][:, :512].rearrange("p (h t) -> p h t", t=128)
                T.op("act", lambda e: e.copy(out=FQ[:, :, pr * 128:(pr + 1) * 128], in_=pv), reads=[("psf", bO)], writes=[("FQ", hd) for hd in range(NH)])
                T.op("act", lambda e: e.activation(out=FE[:, :, pr * 128:(pr + 1) * 128], in_=pv, func=AF.Square),
                     reads=[("psf", bO)], writes=[("FE", hd) for hd in range(NH)])

            SU(0)
            for pr in range(nblk):
                if pr + 1 < nblk:
                    SU(pr + 1)
                O(pr)
            if stop < 6:
                return
            allF = lambda n: [(n, hd) for hd in range(NH)]
            for hd in range(NH):
                b = bank()
                mm_group(psf[b][:, :T_], [(onesVf[:], FE[:, hd, :T_])], reads=[("FE", hd), "onesVf"], writes=[("psf", b)])
                T.op("act", lambda e, hd=hd, b=b: e.activation(out=FS[:, hd, :T_], in_=psf[b][:, :T_], func=AF.Ln, bias=epst[:, 0:1]),
                     reads=[("psf", b), "epst"], writes=[("FS", hd)])
                T.op("act", lambda e, hd=hd: e.activation(out=FS[:, hd, :T_], in_=FS[:, hd, :T_], func=AF.Exp, scale=-0.5),
                     reads=[("FS", hd)], writes=[("FS", hd)])
                T.op("pool", lambda e, hd=hd: e.tensor_tensor(out=FQ[:, hd, :T_], in0=FQ[:, hd, :T_], in1=FS[:, hd, :T_], op=ALU.mult),
                     reads=[("FS", hd), ("FQ", hd)], writes=[("FQ", hd)])
                T.op("dve", lambda e, hd=hd: e.scalar_tensor_tensor(out=BA[:, hd, :T_], in0=FQ[:, hd, :T_], scalar=c_[:, C_OG + hd:C_OG + hd + 1],
                                                                    in1=BG[:, hd, :T_], op0=ALU.mult, op1=ALU.mult),
                     reads=[("FQ", hd), ("BG", hd), ("cst", l)], writes=[("BQ", hd)])
            if stop < 7:
                return
            for g in range(4):
                b = bank()
                mm_group(psf[b][:, :T_], [(pproj[l][:, g, :], BPL[:, g, :T_])], reads=[("pproj", l), ("BD", g)], writes=[("psf", b)])
                T.op("act", lambda e, g=g, b=b: e.mul(out=BP[:, g, :T_], in_=psf[b][:, :T_], mul=c_[:, C_PS + g:C_PS + g + 1]),
                     reads=[("psf", b), ("cst", l)], writes=[("BK", g)])

            if stop < 8:
                return
            for half in range(2):
                sb_ = load_piece(l, 5 if half == 0 else 8)
                Wb = ring[sb_]
                for q4 in range(4):
                    dc = half * 4 + q4
                    cb = q4 * 128
                    bbh, bbp = bank(), bank()
                    mm_group(psf[bbh][:, :T_], [(Wb[:, kc * 512 + cb:kc * 512 + cb + 128], BA[:, kc, :T_]) for kc in range(4)],
                             reads=[("ring", sb_)], writes=[("psf", bbh)], each=[[("BQ", kc)] for kc in range(4)])
                    mm_group(psf[bbp][:, :T_], [(Wb[:, 2048 + kc * 512 + cb:2048 + kc * 512 + cb + 128], BP[:, kc, :T_]) for kc in range(4)],
                             reads=[("ring", sb_)], writes=[("psf", bbp)], each=[[("BK", kc)] for kc in range(4)])
                    za = FQ[:, dc % 2, :T_]
                    zb = FQ[:, 2 + dc % 2, :T_]
                    T.op("dve", lambda e, za=za, bbh=bbh, dc=dc: e.tensor_tensor(out=za, in0=SGA[:, dc, :T_], in1=psf[bbh][:, :T_], op=ALU.mult),
                         reads=[("psf", bbh), ("sqz", dc)], writes=[("FQ", dc % 2)])
                    T.op("dve", lambda e, zb=zb, bbp=bbp, dc=dc: e.tensor_tensor(out=zb, in0=SGB[:, dc, :T_], in1=psf[bbp][:, :T_], op=ALU.mult),
                         reads=[("psf", bbp), ("SGB", dc)], writes=[("FQ", 2 + dc % 2)])
                    T.op("pool", lambda e, za=za, zb=zb, dc=dc: e.tensor_tensor(out=sqz[:, dc, :T_], in0=za, in1=zb, op=ALU.add),
                         reads=[("FQ", dc % 2), ("FQ", 2 + dc % 2)], writes=[("sqz", dc)])
            if stop < 9:
                return
            for half in range(2):
                s = load_piece(l, 11 + half)
                W = ring[s]
                for q4 in range(4):
                    dc = half * 4 + q4
                    b = bank()
                    mm_group(psf[b][:, :T_], [(W[:, kc * 512 + q4 * 128:kc * 512 + (q4 + 1) * 128], sqz[:, kc, :T_]) for kc in range(NDC)],
                             reads=[("ring", s)], writes=[("psf", b)], each=[[("sqz", kc)] for kc in range(NDC)])
                    evac_y(dc, b, T_, u, "u")
            postnorm(l, T_, C_MPOST, u, "u")

        def ffn(l, T_):
            if stop < 10:
                return
            c_ = cst[l]
            prenorm(l, T_, C_FPRE)
            eu = [[("u", kc)] for kc in range(NDC)]

            def mtile(fc):
                blkt = [BQ, BK, BD, BKD, BV, BG][fc // 4]
                return blkt[:, fc % 4, :T_], (["BQ", "BK", "BD", "BKD", "BV", "BG"][fc // 4], fc % 4)

            for j in range(11):
                s = load_piece(l, 13 + j)
                W = ring[s]
                for sub in range(2):
                    fc = 2 * j + sub
                    bg, bu = bank(), bank()
                    mm_group(psf[bg][:, :T_], [(W[:, kc * 256 + sub * 128:kc * 256 + (sub + 1) * 128], u[:, kc, :T_]) for kc in range(NDC)],
                             reads=[("ring", s)], writes=[("psf", bg)], each=eu)
                    mm_group(psf[bu][:, :T_], [(W[:, 2048 + kc * 256 + sub * 128:2048 + kc * 256 + (sub + 1) * 128], u[:, kc, :T_]) for kc in range(NDC)],
                             reads=[("ring", s)], writes=[("psf", bu)], each=eu)
                    gi = fc % 2
                    G = Gb[gi]
                    cv = CV[gi]
                    w0 = c_[:, C_CW + fc:C_CW + fc + 1]
                    w1 = c_[:, C_CW + NFC + fc:C_CW + NFC + fc + 1]
                    w2 = c_[:, C_CW + 2 * NFC + fc:C_CW + 2 * NFC + fc + 1]
                    bb_ = c_[:, C_CB + fc:C_CB + fc + 1]
                    T.op("pool", lambda e, G=G, fc=fc: e.tensor_copy(out=G[:, 0:2], in_=Gh[l][:, fc, :]), reads=[("Gh", l)], writes=[("Gbh", gi)])
                    T.op("act", lambda e, G=G, bg=bg: e.copy(out=G[:, 2:2 + T_], in_=psf[bg][:, :T_]), reads=[("psf", bg)], writes=[("Gb", gi)])
                    T.op("act", lambda e, cv=cv, bg=bg, w2=w2, bb_=bb_: e.activation(out=cv[:, :T_], in_=psf[bg][:, :T_], func=AF.Identity, scale=w2, bias=bb_),
                         reads=[("psf", bg), ("cst", l)], writes=[("CV", gi)])
                    T.op("pool", lambda e, G=G, fc=fc: e.tensor_copy(out=Gh[l][:, fc, :], in_=G[:, T_:T_ + 2]), reads=[("Gb", gi)], writes=[("Gh", l)])
                    T.op("dve", lambda e, G=G, cv=cv, w1=w1: e.scalar_tensor_tensor(out=cv[:, :T_], in0=G[:, 1:1 + T_], scalar=w1, in1=cv[:, :T_],
                                                                                  op0=ALU.mult, op1=ALU.add),
                         reads=[("Gb", gi), ("Gbh", gi), ("CV", gi), ("cst", l)], writes=[("CV", gi)])
                    T.op("dve", lambda e, G=G, cv=cv, w0=w0: e.scalar_tensor_tensor(out=cv[:, :T_], in0=G[:, 0:T_], scalar=w0, in1=cv[:, :T_],
                                                                                  op0=ALU.mult, op1=ALU.add),
                         reads=[("Gb", gi), ("Gbh", gi), ("CV", gi), ("cst", l)], writes=[("CV", gi)])
                    T.op("act", lambda e, cv=cv: e.activation(out=cv[:, :T_], in_=cv[:, :T_], func=AF.Gelu_apprx_tanh),
                         reads=[("CV", gi)], writes=[("CV", gi)])
                    mt, mres = mtile(fc)
                    T.op("dve", lambda e, cv=cv, mt=mt, bu=bu: e.tensor_tensor(out=mt, in0=cv[:, :T_], in1=psf[bu][:, :T_], op=ALU.mult),
                         reads=[("CV", gi), ("psf", bu)], writes=[mres])
            for dc in range(NDC):
                s = load_piece(l, 24 + dc)
                W = ring[s]
                b = bank()
                pairs = []
                each = []
                for fc in range(NFC):
                    mt, mres = mtile(fc)
                    pairs.append((W[:, fc * 128:(fc + 1) * 128], mt))
                    each.append([mres])
                mm_group(psf[b][:, :T_], pairs, reads=[("ring", s)], writes=[("psf", b)], each=each)
                evac_y(dc, b, T_, sqz, "sqz")
            postnorm(l, T_, C_FPOST, sqz, "sqz")

        def tokhalf(hf, store=False):
            if store:
                return (FS if hf == 0 else FK), ("FS" if hf == 0 else "FK")
            return (FB if hf == 0 else FE), ("FB" if hf == 0 else "FE")

        def load_dma(src_rows, T_):
            nblk = T_ // 128
            srcv = src_rows.rearrange("(b p) (hf f) -> p b hf f", p=128, hf=2)
            for hf in range(2):
                tb, tn = tokhalf(hf)
                T.dma("sp", tb[:, :nblk, :], srcv[:, :, hf, :], writes=[(tn, k) for k in range(4)], key=("in", hf))

        def load_tile(T_):
            nblk = T_ // 128
            for dc in range(NDC):
                tb, tn = tokhalf(dc // 4)
                b = bank()
                for blk in range(nblk):
                    T.op("pe", lambda e, tb=tb, dc=dc, blk=blk, b=b: e.transpose(out=psf[b][:, blk * 128:(blk + 1) * 128],
                                                                                 in_=tb[:, blk, (dc % 4) * 128:(dc % 4 + 1) * 128], identity=identf[:]),
                         reads=[(tn, blk), "identf"], writes=[("psf", b)], signal=(blk == nblk - 1))
                if dc % 2 == 0:
                    T.op("act", lambda e, dc=dc, b=b: e.copy(out=h[:, dc, :T_], in_=psf[b][:, :T_]), reads=[("psf", b)], writes=[("h", dc)])
                else:
                    T.op("dve", lambda e, dc=dc, b=b: e.tensor_copy(out=h[:, dc, :T_], in_=psf[b][:, :T_]), reads=[("psf", b)], writes=[("h", dc)])

        def store_tile(dst_rows, T_):
            nblk = T_ // 128
            dstv = dst_rows.rearrange("(b p) (hf f) -> p b hf f", p=128, hf=2)
            toks = []
            for hf in range(2):
                tb, tn = tokhalf(hf, store=True)
                for blk in range(nblk):
                    b = bank()
                    for q4 in range(4):
                        dc = hf * 4 + q4
                        T.op("pe", lambda e, dc=dc, blk=blk, b=b, q4=q4: e.transpose(out=psf[b][:, q4 * 128:(q4 + 1) * 128],
                                                                                   in_=h[:, dc, blk * 128:(blk + 1) * 128], identity=identf[:]),
                             reads=[("h", dc), "identf"], writes=[("psf", b)], signal=(q4 == 3))
                    if blk % 2 == 0:
                        T.op("act", lambda e, tb=tb, blk=blk, b=b: e.copy(out=tb[:, blk, :], in_=psf[b][:, :512]), reads=[("psf", b)], writes=[(tn, blk)])
                    else:
                        T.op("dve", lambda e, tb=tb, blk=blk, b=b: e.tensor_copy(out=tb[:, blk, :], in_=psf[b][:, :512]), reads=[("psf", b)], writes=[(tn, blk)])
                toks.append(T.dma("sp", dstv[:, :, hf, :], tb[:, :nblk, :], reads=[(tn, k) for k in range(4)], key=("out", hf)))
            return toks

        last = []
        seq1 = [("meta", None, T0)] + [("x1", t, TT) for t in range(k1x)]
        seq2 = [("x2", t, TT) for t in range(k2)]

        def src_of(item):
            nm, t, T_ = item
            return dr[nm] if t is None else dr[nm][t * TT:(t + 1) * TT, :]

        items = [(1, it) for it in seq1] + [(2, it) for it in seq2]
        load_dma(src_of(items[0][1]), items[0][1][2])
        first2 = True
        pending = [None]
        for idx, (ph, it) in enumerate(items):
            T_ = it[2]
            last_prefix = (ph == 1 and idx == len(seq1) - 1)
            is_meta = (ph == 1 and idx == 0) or last_prefix
            if last_prefix:
                T.dma("sp", invc[:], dr["invc2"], writes=["invc"], key="c3")
            load_tile(T_)
            if ph == 1 and idx >= 1 and deferred:
                nper = -(-30 // max(1, (k1x - 2)))
                for _ in range(min(nper, len(deferred))):
                    cast_piece(*deferred.pop(0))
            for l in range(nlayers):
                so = (ph == 1 and l == nlayers - 1 and not last_prefix and nlayers > 1)
                if so:
                    pending[0] = mixer(l, T_, is_meta, state_only=True)
                else:
                    hook, pending[0] = (pending[0], None) if l == 0 else (None, pending[0])
                    mixer(l, T_, is_meta, mid_hook=hook)
                if l == nlayers - 1 and idx + 1 < len(items):
                    load_dma(src_of(items[idx + 1][1]), items[idx + 1][1][2])
                if not so:
                    ffn(l, T_)
            if ph == 2:
                first2 = False
                last = store_tile(out[it[1] * TT:(it[1] + 1) * TT, :], T_)
        if pending[0] is not None:
            pending[0]()
        assert not deferred
        for tk in last:
            T.wait_tok("sp", tk)

        with nc.Block() as block:
            @block.sync
            def _(e):
                for th in T.thunks["sp"]:
                    th(e)

            @block.tensor
            def _(e):
                for th in T.thunks["pe"]:
                    th(e)

            @block.scalar
            def _(e):
                for th in T.thunks["act"]:
                    th(e)

            @block.vector
            def _(e):
                for th in T.thunks["dve"]:
                    th(e)

            @block.gpsimd
            def _(e):
                for th in T.thunks["pool"]:
                    th(e)
    return nc


def host_consts():
    ident = np.eye(128, dtype=np.float32)
    s = np.arange(128)[:, None]
    t = np.arange(128)[None, :]
    trimask = ((s // 64 == t // 64) & (s <= t)).astype(np.float32)
    scanmask = np.ones((128, TT), np.float32)
    scanmask[:, ::64] = 0.0

    def table(width, with_meta):
        tb = np.ones((128, 4, width), np.float32)
        for g, w in enumerate((2, 4, 8, 16)):
            if with_meta:
                pos = np.arange(width) - (width - NMETA) + 1
                cnt = np.minimum(np.maximum(pos, 1), w).astype(np.float32)
            else:
                cnt = np.full(width, w, np.float32)
            tb[:, g, :] = (1.0 / cnt)[None, :]
        return tb
    return ident, trimask, scanmask, table


def make_cst(inp):
    cst = np.zeros((L, 128, NCST), np.float32)
    lbr = np.asarray(inp["hgrn_lower_bounds"], np.float32)
    for l in range(L):
        cst[l, :, C_R0:C_R0 + 4] = lbr[0].reshape(4, 128).T
        cst[l, :, C_R1:C_R1 + 4] = lbr[1].reshape(4, 128).T
        cst[l, :, C_OG:C_OG + 4] = np.asarray(inp["hgrn_out_norm"][l]).reshape(4, 128).T
        cst[l, :, C_PS:C_PS + 4] = np.asarray(inp["pool_scale"][l]).reshape(4, 128).T
        cst[l, :, C_MPRE:C_MPRE + 8] = np.asarray(inp["mix_norm_pre"][l]).reshape(8, 128).T
        cst[l, :, C_MPOST:C_MPOST + 8] = np.asarray(inp["mix_norm_post"][l]).reshape(8, 128).T
        cst[l, :, C_FPRE:C_FPRE + 8] = np.asarray(inp["ffn_norm_pre"][l]).reshape(8, 128).T
        cst[l, :, C_FPOST:C_FPOST + 8] = np.asarray(inp["ffn_norm_post"][l]).reshape(8, 128).T
        cw = np.asarray(inp["ffn_conv_w"][l])
        for j in range(3):
            cst[l, :, C_CW + j * NFC:C_CW + (j + 1) * NFC] = cw[j].reshape(NFC, 128).T
        cst[l, :, C_CB:C_CB + NFC] = np.asarray(inp["ffn_conv_b"][l]).reshape(NFC, 128).T
    return cst


def make_in_maps(inp, k1x, k2, batches, nx_total=SEQ):
    ident, trimask, scanmask, table = host_consts()
    cst = make_cst(inp)
    meta = np.zeros((T0, D), np.float32)
    meta[T0 - NMETA:] = np.asarray(inp["meta_tokens"], np.float32)
    meta512 = np.zeros((TT, D), np.float32)
    meta512[TT - NMETA:] = np.asarray(inp["meta_tokens"], np.float32)
    f = lambda k: np.ascontiguousarray(np.asarray(inp[k], np.float32))
    shared = {
        "w_in": f("w_in"), "w_bh": f("w_branch_hgrn"), "w_bp": f("w_branch_pool"), "w_out": f("w_out"),
        "w_gate": f("ffn_w_gate"), "w_up": f("ffn_w_up"), "w_down": f("ffn_w_down"), "pproj": f("pool_proj"),
        "cst": cst, "ident": ident, "trimask": trimask, "scanmask": scanmask,
    }
    x = np.asarray(inp["x"], np.float32)
    mapsA, mapsB = [], []
    for b in batches:
        xb = x[b, :nx_total]
        a = dict(shared)
        a["meta"] = np.zeros((T0, D), np.float32)
        x1 = np.zeros((k1x * TT, D), np.float32)
        x1[(k1x - 1) * TT:] = meta512
        a["x1"] = x1
        x2 = np.zeros((k2 * TT, D), np.float32)
        na = min(k2 * TT, nx_total)
        x2[:na] = xb[:na]
        a["x2"] = x2
        a["invc1"] = table(T0, False)
        a["invc2"] = table(TT, True)
        mapsA.append(a)
        bm = dict(shared)
        bm["meta"] = meta
        bm["x1"] = np.ascontiguousarray(xb[:k1x * TT])
        x2 = np.zeros((k2 * TT, D), np.float32)
        nb = nx_total - k1x * TT
        x2[:nb] = xb[k1x * TT:]
        bm["x2"] = x2
        bm["invc1"] = table(T0, True)
        bm["invc2"] = table(TT, False)
        mapsB.append(bm)
    return mapsA + mapsB


K1X = 8
K2 = 8


def kernel(**inputs):
    nc = build_nc(K1X, K2, L)
    maps = make_in_maps(inputs, K1X, K2, [0, 1, 2, 3])
    order = []
    for b in range(4):
        order += [maps[b], maps[4 + b]]
    res = run_bass_kernel_spmd(nc, order, core_ids=list(range(8)))
    out = np.empty((4, SEQ, D), np.float32)
    for b in range(4):
        oa = np.asarray(res.results[2 * b]["out"], np.float32)
        ob = np.asarray(res.results[2 * b + 1]["out"], np.float32)
        out[b, :K2 * TT] = oa[:K2 * TT]
        out[b, K1X * TT:] = ob[:SEQ - K1X * TT]
    return out
```
